# Optimizing a Trainium2 kernel written in Bass

```python
import math
import jax, jax.numpy as jnp
from jax import lax
import numpy as np

D_MODEL = 1024
BATCH = 8
SEQ = 2048
DEPTH = 1

RMS_EPS = 1e-6
NEG_INF = -1e30

S5_WIDTH = D_MODEL // 2
S5_GROUP = 16
S5_GROUPS = S5_WIDTH // S5_GROUP
S5_STATE = 64

N_Q_HEADS = 16
N_KV_HEADS = 4
HEAD_DIM = 64
Q_PER_KV = N_Q_HEADS // N_KV_HEADS
WINDOW = 128
ATTN_BLOCK = 128
N_BUCKETS = 32
MAX_DISTANCE = 128

PEER_HEADS = 8
PEER_KEY_DIM = 128
N_SUB_KEYS = 128
N_EXPERTS = N_SUB_KEYS * N_SUB_KEYS
PEER_TOPK = 16
PEER_TOKEN_BLOCK = 128

Q_WIDTH = N_Q_HEADS * HEAD_DIM
KV_WIDTH = N_KV_HEADS * HEAD_DIM
GATE_WIDTH = 2 * D_MODEL
IN_WIDTH = S5_WIDTH + Q_WIDTH + 2 * KV_WIDTH + GATE_WIDTH

kernel_name = "hybrid_s5_swa_peer_block"


def rms_norm(x, g):
    xf = x.astype(jnp.float32)
    r = lax.rsqrt(jnp.mean(xf * xf, axis=-1, keepdims=True) + RMS_EPS)
    return (xf * r).astype(x.dtype) * g


def _complex_scan_combine(c1, c2):
    a1r, a1i, b1r, b1i = c1
    a2r, a2i, b2r, b2i = c2
    ar = a2r * a1r - a2i * a1i
    ai = a2r * a1i + a2i * a1r
    br = a2r * b1r - a2i * b1i + b2r
    bi = a2r * b1i + a2i * b1r + b2i
    return (ar, ai, br, bi)


def s5_mixer(u, lam_re, lam_im, log_step, b_re, b_im, c_re, c_im, d_skip, w_glu):
    bsz, seq, _ = u.shape
    f32 = jnp.float32
    uf = u.astype(f32).reshape(bsz, seq, S5_GROUPS, S5_GROUP)
    lr, li = lam_re.astype(f32), lam_im.astype(f32)
    step = jnp.exp(log_step.astype(f32))[:, None]
    mag = jnp.exp(lr * step)
    abar_re = mag * jnp.cos(li * step)
    abar_im = mag * jnp.sin(li * step)
    den = lr * lr + li * li
    nr, ni = abar_re - 1.0, abar_im
    coef_re = ((nr * lr + ni * li) / den)[..., None]
    coef_im = ((ni * lr - nr * li) / den)[..., None]
    br, bi = b_re.astype(f32), b_im.astype(f32)
    bbar_re = coef_re * br - coef_im * bi
    bbar_im = coef_re * bi + coef_im * br
    bu_re = jnp.einsum('blgc,gpc->blgp', uf, bbar_re)
    bu_im = jnp.einsum('blgc,gpc->blgp', uf, bbar_im)
    a_re = jnp.broadcast_to(abar_re, bu_re.shape)
    a_im = jnp.broadcast_to(abar_im, bu_im.shape)
    _, _, h_re, h_im = lax.associative_scan(
        _complex_scan_combine, (a_re, a_im, bu_re, bu_im), axis=1)
    y = (jnp.einsum('blgp,gcp->blgc', h_re, c_re.astype(f32))
         - jnp.einsum('blgp,gcp->blgc', h_im, c_im.astype(f32))
         + d_skip.astype(f32) * uf)
    y = jax.nn.gelu(y.reshape(bsz, seq, S5_WIDTH), approximate=False).astype(u.dtype)
    ab = y @ w_glu
    a, b = jnp.split(ab, 2, axis=-1)
    return a * jax.nn.sigmoid(b)


def t5_bucket(dist):
    max_exact = N_BUCKETS // 2
    d_f = jnp.maximum(dist, 1).astype(jnp.float32)
    large = max_exact + (jnp.log(d_f / max_exact) / math.log(MAX_DISTANCE / max_exact)
                         * (N_BUCKETS - max_exact)).astype(jnp.int32)
    large = jnp.minimum(large, N_BUCKETS - 1)
    return jnp.where(dist < max_exact, dist, large)


def sliding_window_attention(q, k, v, q_norm_g, k_norm_g, rel_bias_table, sinks):
    bsz, seq, _ = q.shape
    nb = seq // ATTN_BLOCK
    q = rms_norm(q.reshape(bsz, seq, N_KV_HEADS, Q_PER_KV, HEAD_DIM), q_norm_g)
    k = rms_norm(k.reshape(bsz, seq, N_KV_HEADS, HEAD_DIM), k_norm_g)
    v = v.reshape(bsz, seq, N_KV_HEADS, HEAD_DIM)
    qb = q.reshape(bsz, nb, ATTN_BLOCK, N_KV_HEADS, Q_PER_KV, HEAD_DIM)

    def band(t):
        pad = jnp.pad(t, ((0, 0), (ATTN_BLOCK, 0), (0, 0), (0, 0)))
        prev = pad[:, :seq].reshape(bsz, nb, ATTN_BLOCK, N_KV_HEADS, HEAD_DIM)
        cur = t.reshape(bsz, nb, ATTN_BLOCK, N_KV_HEADS, HEAD_DIM)
        return jnp.concatenate([prev, cur], axis=2)

    kb, vb = band(k), band(v)
    scale = HEAD_DIM ** -0.5
    scores = jnp.einsum('bnqkgd,bnskd->bnkgqs', qb, kb).astype(jnp.float32) * scale

    qi = jnp.arange(ATTN_BLOCK)[:, None]
    si = jnp.arange(2 * ATTN_BLOCK)[None, :]
    dist = ATTN_BLOCK + qi - si
    band_ok = (dist >= 0) & (dist < WINDOW)
    bias = rel_bias_table.astype(jnp.float32)[t5_bucket(jnp.maximum(dist, 0))]
    bias = bias.transpose(2, 0, 1).reshape(N_KV_HEADS, Q_PER_KV, ATTN_BLOCK, 2 * ATTN_BLOCK)
    blk = jnp.arange(nb)[:, None, None]
    key_ok = (blk - 1) * ATTN_BLOCK + si[None] >= 0
    valid = (band_ok[None] & key_ok)[None, :, None, None]

    scores = jnp.where(valid, scores + bias, NEG_INF)
    sink = sinks.astype(jnp.float32).reshape(1, 1, N_KV_HEADS, Q_PER_KV, 1, 1)
    m = jnp.maximum(jnp.max(scores, axis=-1, keepdims=True), sink)
    p = jnp.exp(scores - m)
    p = p / (jnp.sum(p, axis=-1, keepdims=True) + jnp.exp(sink - m))
    out = jnp.einsum('bnkgqs,bnskd->bnqkgd', p.astype(vb.dtype), vb)
    return out.reshape(bsz, seq, Q_WIDTH)


def peer_mixer(xn, w_query, sub_keys, u_tab, v_tab):
    bsz, seq, d = xn.shape
    t = bsz * seq
    xt = xn.reshape(t, d)
    q = (xt @ w_query).reshape(t, PEER_HEADS, 2, PEER_KEY_DIM // 2)
    s = jnp.einsum('thkd,hknd->thkn', q, sub_keys).astype(jnp.float32)
    vals, idx = lax.top_k(s, PEER_TOPK)
    cand = (vals[:, :, 0, :, None] + vals[:, :, 1, None, :]).reshape(t, PEER_HEADS, PEER_TOPK * PEER_TOPK)
    best, pos = lax.top_k(cand, PEER_TOPK)
    i1 = jnp.take_along_axis(idx[:, :, 0], pos // PEER_TOPK, axis=-1)
    i2 = jnp.take_along_axis(idx[:, :, 1], pos % PEER_TOPK, axis=-1)
    expert = (i1 * N_SUB_KEYS + i2).reshape(t, PEER_HEADS * PEER_TOPK)
    gate = jax.nn.softmax(best, axis=-1).reshape(t, PEER_HEADS * PEER_TOPK)

    def block_fn(args):
        xb, eb, gb = args
        a = jnp.einsum('cd,ced->ce', xb, u_tab[eb]).astype(jnp.float32)
        w = (gb * jax.nn.gelu(a, approximate=False)).astype(xb.dtype)
        return jnp.einsum('ce,ced->cd', w, v_tab[eb])

    nblk = t // PEER_TOKEN_BLOCK
    out = lax.map(block_fn, (xt.reshape(nblk, PEER_TOKEN_BLOCK, d),
                             expert.reshape(nblk, PEER_TOKEN_BLOCK, -1),
                             gate.reshape(nblk, PEER_TOKEN_BLOCK, -1)))
    return out.reshape(bsz, seq, d)


def setup_inputs(seed: int = 0) -> dict:
    key = jax.random.key(seed)
    ks = jax.random.split(key, 26)
    f32 = jnp.float32
    L, D, G, P, C = DEPTH, D_MODEL, S5_GROUPS, S5_STATE, S5_GROUP
    nrm = lambda k, shape, s: jax.random.normal(k, shape, f32) * s
    inp = {}
    inp['x'] = nrm(ks[0], (BATCH, SEQ, D), 1.0)
    inp['ln1_g'] = 1.0 + nrm(ks[1], (L, D), 0.02)
    inp['w_in'] = nrm(ks[2], (L, D, IN_WIDTH), D ** -0.5)
    inp['b_gate'] = nrm(ks[3], (L, GATE_WIDTH), 0.02)
    inp['s5_lambda_re'] = -0.5 + nrm(ks[4], (L, G, P), 0.01)
    inp['s5_lambda_im'] = jnp.pi * jnp.arange(P, dtype=f32) + nrm(ks[5], (L, G, P), 0.01)
    inp['s5_log_step'] = jax.random.uniform(ks[6], (L, G), f32, math.log(1e-3), math.log(1e-1))
    inp['s5_b_re'] = nrm(ks[7], (L, G, P, C), (2 * C) ** -0.5)
    inp['s5_b_im'] = nrm(ks[8], (L, G, P, C), (2 * C) ** -0.5)
    inp['s5_c_re'] = nrm(ks[9], (L, G, C, P), P ** -0.5)
    inp['s5_c_im'] = nrm(ks[10], (L, G, C, P), P ** -0.5)
    inp['s5_d'] = nrm(ks[11], (L, G, C), 1.0)
    inp['s5_w_glu'] = nrm(ks[12], (L, S5_WIDTH, 2 * S5_WIDTH), S5_WIDTH ** -0.5)
    inp['w_s5_branch'] = nrm(ks[13], (L, S5_WIDTH, D), S5_WIDTH ** -0.5)
    inp['q_norm_g'] = 1.0 + nrm(ks[14], (L, HEAD_DIM), 0.02)
    inp['k_norm_g'] = 1.0 + nrm(ks[15], (L, HEAD_DIM), 0.02)
    inp['rel_bias_table'] = nrm(ks[16], (N_BUCKETS, N_Q_HEADS), 0.1)
    inp['attn_sinks'] = nrm(ks[17], (L, N_Q_HEADS), 0.5)
    inp['w_attn_branch'] = nrm(ks[18], (L, Q_WIDTH, D), Q_WIDTH ** -0.5)
    inp['w_out'] = nrm(ks[19], (L, D, D), D ** -0.5)
    inp['ln2_g'] = 1.0 + nrm(ks[20], (L, D), 0.02)
    inp['peer_w_query'] = nrm(ks[21], (L, D, PEER_HEADS * PEER_KEY_DIM), D ** -0.5)
    inp['peer_sub_keys'] = nrm(ks[22], (L, PEER_HEADS, 2, N_SUB_KEYS, PEER_KEY_DIM // 2), (PEER_KEY_DIM // 2) ** -0.5)
    inp['peer_u'] = nrm(ks[23], (L, N_EXPERTS, D), D ** -0.5)
    inp['peer_v'] = nrm(ks[24], (L, N_EXPERTS, D), D ** -0.5)
    return inp


def reference(x, ln1_g, w_in, b_gate, s5_lambda_re, s5_lambda_im, s5_log_step,
              s5_b_re, s5_b_im, s5_c_re, s5_c_im, s5_d, s5_w_glu, w_s5_branch,
              q_norm_g, k_norm_g, rel_bias_table, attn_sinks, w_attn_branch, w_out,
              ln2_g, peer_w_query, peer_sub_keys, peer_u, peer_v):
    splits = [S5_WIDTH, S5_WIDTH + Q_WIDTH, S5_WIDTH + Q_WIDTH + KV_WIDTH,
              S5_WIDTH + Q_WIDTH + 2 * KV_WIDTH]
    h = x
    for layer in range(DEPTH):
        xn = rms_norm(h, ln1_g[layer])
        proj = xn @ w_in[layer]
        u_s5, q, k, v, gate_logits = jnp.split(proj, splits, axis=-1)
        gates = jax.nn.sigmoid(gate_logits + b_gate[layer])
        g_s5, g_attn = jnp.split(gates, 2, axis=-1)
        y_s5 = s5_mixer(u_s5, s5_lambda_re[layer], s5_lambda_im[layer], s5_log_step[layer],
                        s5_b_re[layer], s5_b_im[layer], s5_c_re[layer], s5_c_im[layer],
                        s5_d[layer], s5_w_glu[layer]) @ w_s5_branch[layer]
        y_attn = sliding_window_attention(q, k, v, q_norm_g[layer], k_norm_g[layer],
                                          rel_bias_table, attn_sinks[layer]) @ w_attn_branch[layer]
        h = h + (g_s5 * y_s5 + g_attn * y_attn) @ w_out[layer]
        xn2 = rms_norm(h, ln2_g[layer])
        h = h + peer_mixer(xn2, peer_w_query[layer], peer_sub_keys[layer],
                           peer_u[layer], peer_v[layer])
    return h
```

```python
import contextlib
import numpy as np
import concourse.bass as bass
import concourse.mybir as mybir
from concourse.bass_utils import run_bass_kernel_spmd

F32 = mybir.dt.float32
BF16 = mybir.dt.bfloat16
I32 = mybir.dt.int32
U32 = mybir.dt.uint32
ALU = mybir.AluOpType
AF = mybir.ActivationFunctionType
AX = mybir.AxisListType


class Sched:
    ENG = ("pe", "act", "dve", "pool", "sp")

    def __init__(self, nc, stack):
        self.nc = nc
        self.stack = stack
        self.obj = {"pe": nc.tensor, "act": nc.scalar, "dve": nc.vector,
                    "pool": nc.gpsimd, "sp": nc.sync}
        self.sems = {}
        self.count = {}
        self.lastw = {}
        self.readers = {}
        self.waited = {e: {} for e in self.ENG}
        self.hist = {e: {} for e in self.ENG}
        self.seq = {e: 0 for e in self.ENG}
        self.nwaits = 0
        self.streams = {e: [] for e in self.ENG}
        self.nops = 0
        self.nobarrier = set()
        self.deferred = None
        self.interleave = None

    def defer_begin(self):
        self.deferred = []

    def defer_end(self):
        d, self.deferred = self.deferred, None
        return d

    def flush(self, lst, n):
        for _ in range(min(n, len(lst))):
            e, fn, r, w, dma = lst.pop(0)
            self.add(e, fn, r, w, dma)
        while lst and getattr(lst[0][1], "_sticky", False):
            e, fn, r, w, dma = lst.pop(0)
            self.add(e, fn, r, w, dma)

    def _sem(self, key):
        if key not in self.sems:
            self.sems[key] = self.stack.enter_context(self.nc.semaphore(key))
            self.count[key] = 0
        return self.sems[key]

    def _set(self, e, k, v):
        if self.waited[e].get(k, 0) < v:
            self.waited[e][k] = v
            self.hist[e].setdefault(k, []).append((self.seq[e], v))

    def _learn(self, e, pe_, pseq):
        for k2, lst in self.hist[pe_].items():
            lo, hi = 0, len(lst)
            while lo < hi:
                mid = (lo + hi) // 2
                if lst[mid][0] <= pseq:
                    lo = mid + 1
                else:
                    hi = mid
            if lo > 0:
                self._set(e, k2, lst[lo - 1][1])

    def add(self, e, fn, r=(), w=(), dma=None):
        if self.deferred is not None:
            self.deferred.append((e, fn, tuple(r), tuple(w), dma))
            return None
        self.seq[e] += 1
        deps = []
        for x in r:
            if x in self.lastw:
                deps.append((self.lastw[x], "raw"))
        for x in w:
            if x in self.lastw:
                deps.append((self.lastw[x], "waw"))
            for rd in self.readers.get(x, ()):
                deps.append((rd, "war"))
        need = {}
        for (pe_, key, val, isdma, pseq), kind in deps:
            if pe_ == e and not isdma:
                if e == "pe":
                    continue
            if self.waited[e].get(key, 0) >= val:
                continue
            if key not in need or need[key][0] < val:
                need[key] = (val, pe_, pseq, isdma)
        waits = {}
        for key in sorted(need, key=lambda k: (need[k][3], k)):
            val, pe_, pseq, isdma = need[key]
            if self.waited[e].get(key, 0) >= val:
                continue
            waits[key] = val
            self._set(e, key, val)
            self._learn(e, pe_, pseq)
        if dma is not None:
            key, inc = dma, 16
        else:
            key, inc = "E_" + e, 1
        self._sem(key)
        self.count[key] += inc
        tok = (e, key, self.count[key], dma is not None, self.seq[e])
        for x in w:
            self.lastw[x] = tok
            self.readers[x] = []
        for x in r:
            self.readers.setdefault(x, []).append(tok)
        self.streams[e].append((list(waits.items()), fn, key, inc))
        self.nops += 1
        self.nwaits += len(waits)
        if self.interleave is not None:
            lst, n = self.interleave
            self.interleave = None
            self.flush(lst, n)
            self.interleave = (lst, n)
        return tok

    def barrier(self):
        for e in self.ENG:
            waits = []
            self.seq[e] += 1
            for k, v in self.count.items():
                if k in self.nobarrier:
                    continue
                if v > 0 and self.waited[e].get(k, 0) < v:
                    waits.append((k, v))
                    self._set(e, k, v)
            if waits:
                self.streams[e].append((sorted(waits), None, None, 0))

    def emit(self):
        self.barrier()
        nc = self.nc
        names = {"pe": "tensor", "act": "scalar", "dve": "vector", "pool": "gpsimd", "sp": "sync"}
        with nc.Block() as block:
            for e in self.ENG:
                stream = self.streams[e]

                def body(eng, stream=stream):
                    for waits, fn, key, inc in stream:
                        for k, v in waits:
                            eng.wait_ge(self.sems[k], v)
                        if fn is not None:
                            fn(eng).then_inc(self.sems[key], inc)

                getattr(block, names[e])(body)
        self.streams = {e: [] for e in self.ENG}


D = 1024
T = 2048
NB = T // 128
NCH = T // 512
NEXP = 16384
EPS = 1e-6
TWO_PI = float(2 * np.pi)
NEG = -30000.0


def build_nc(dbg=False, stop_after=None):
    nc = bass.Bass("TRN2", target_bir_lowering=False)

    def din(name, shape, dt=F32):
        return nc.dram_tensor(name, list(shape), dt, kind="ExternalInput").ap()

    def dout(name, shape, dt=F32):
        return nc.dram_tensor(name, list(shape), dt, kind="ExternalOutput").ap()

    x = din("x", [T, D])
    g1col = din("g1col", [128, 8])
    w_in = din("w_in", [D, 4096])
    bgcol = din("bgcol", [128, 16])
    lr_row = din("lr_row", [128, 2048]); li_row = din("li_row", [128, 2048]); ls_row = din("ls_row", [128, 2048])
    lr_col = din("lr_col", [128, 16]); li_col = din("li_col", [128, 16]); ls_col = din("ls_col", [128, 16])
    bT_re = din("bT_re", [128, 2048]); bT_im = din("bT_im", [128, 2048])
    cpad_re = din("cpad_re", [128, 16 * 128]); cpad_im = din("cpad_im", [128, 16 * 128])
    ddiag = din("ddiag", [128, 4 * 128])
    w_glu = din("w_glu", [512, 1024]); w_s5b = din("w_s5b", [512, 1024])
    qgcol = din("qgcol", [128, 1]); kgcol = din("kgcol", [128, 1])
    sinks = din("sinks", [128, 16])
    bias_t = din("bias_t", [128, 4, 2 * 4 * 128])
    mask_t = din("mask_t", [128, 2 * 4 * 128])
    w_attn = din("w_attn", [D, D]); w_out = din("w_out", [D, D])
    g2row = din("g2row", [128, D])
    w_pq = din("w_pq", [D, D])
    skbd = din("skbd", [128, 8 * 256])
    peer_u = din("peer_u", [NEXP, D]); peer_v = din("peer_v", [NEXP, D])
    ident = din("ident", [128, 128]); tri = din("tri", [128, 128]); blk1 = din("blk1", [128, 128])
    iota_col = din("iota_col", [128, 1]); iota_row = din("iota_row", [128, 128])
    out = dout("out", [T, D])
    dbgo = {}
    if dbg:
        dbgo["xnT"] = dout("d_xnT", [128, 8 * T])
        dbgo["yg"] = dout("d_yg", [128, 4 * T])
        dbgo["mix"] = dout("d_mix", [128, 8 * T])
        dbgo["ao"] = dout("d_ao", [128, 8 * T])
        dbgo["h"] = dout("d_h", [128, NB * D])

    with contextlib.ExitStack() as st:
        S = Sched(nc, st)

        def sbuf(stack, name, shape, dt=F32):
            return stack.enter_context(nc.sbuf_tensor(name, list(shape), dt))

        def dma(q, o, i, r=(), w=(), key=None):
            S.add(q, lambda e: e.dma_start(out=o, in_=i), r=r, w=w, dma=key)

        def mm(o, lhsT, rhs, start, stop, r=(), w=()):
            S.add("pe", lambda e: e.matmul(o, lhsT=lhsT, rhs=rhs, start=start, stop=stop), r=r, w=w)

        def tr(o, i, r=(), w=()):
            S.add("pe", lambda e: e.transpose(out=o, in_=i, identity=id_bf[:]), r=r, w=w)

        def act(o, i, func, r=(), w=(), bias=None, scale=None, accum=None, sticky=False):
            kw = {}
            if bias is not None:
                kw["bias"] = bias
            if scale is not None:
                kw["scale"] = scale
            if accum is not None:
                kw["accum_out"] = accum
            fn = lambda e: e.activation(out=o, in_=i, func=func, **kw)
            if sticky:
                fn._sticky = True
            S.add("act", fn, r=r, w=w)

        def tt(q, o, a, b, op, r=(), w=()):
            S.add(q, lambda e: e.tensor_tensor(out=o, in0=a, in1=b, op=op), r=r, w=w)

        def ts(q, o, a, s1, s2, op0, op1=None, r=(), w=(), accum=None):
            kw = {}
            if op1 is not None:
                kw["op1"] = op1
            if accum is not None:
                kw["accum_out"] = accum
            S.add(q, lambda e: e.tensor_scalar(out=o, in0=a, scalar1=s1, scalar2=s2, op0=op0, **kw), r=r, w=w)

        def stt(q, o, a, sc, b, op0, op1, r=(), w=(), accum=None):
            kw = {}
            if accum is not None:
                kw["accum_out"] = accum
            S.add(q, lambda e: e.scalar_tensor_tensor(out=o, in0=a, scalar=sc, in1=b, op0=op0, op1=op1, **kw), r=r, w=w)

        def cp(q, o, i, r=(), w=()):
            if q == "act":
                S.add(q, lambda e: e.copy(out=o, in_=i), r=r, w=w)
            else:
                S.add(q, lambda e: e.tensor_copy(out=o, in_=i), r=r, w=w)

        def wload(dst, src3, nk, w, key):
            tok = None
            for k in range(nk):
                o_, i_ = dst[:, k, :], src3[:, k, :]
                tok = S.add("pool", (lambda e, o_=o_, i_=i_: e.dma_start(out=o_, in_=i_)), w=["%s.%d" % (w[0], k)], dma=key)
            S.lastw[w[0]] = tok
            S.readers[w[0]] = []

        def recip(o, i, r=(), w=()):
            S.add("dve", lambda e: e.reciprocal(out=o, in_=i), r=r, w=w)

        def memset(q, o, val, w=()):
            S.add(q, lambda e: e.memset(o, val), w=w)

        def rsqrt_chain(o, i, scale, r, w, tag):
            ts("dve", o, i, scale, EPS, ALU.mult, ALU.add, r=r, w=w)
            act(o, o, AF.Sqrt, r=w, w=w)
            recip(o, o, r=w, w=w)

        def sincos(sin_o, cos_o, ang, tmpf, tmpi, rs, tag):
            ts("dve", tmpf, ang, 1.0 / TWO_PI, None, ALU.mult, r=rs, w=[tag + "tf"])
            cp("dve", tmpi, tmpf, r=[tag + "tf"], w=[tag + "ti"])
            cp("dve", tmpf, tmpi, r=[tag + "ti"], w=[tag + "tf"])
            stt("dve", tmpf, tmpf, -TWO_PI, ang, ALU.mult, ALU.add, r=[tag + "tf"] + list(rs), w=[tag + "tf"])
            ts("dve", tmpf, tmpf, float(np.pi), float(-np.pi), ALU.min, ALU.max, r=[tag + "tf"], w=[tag + "tf"])
            act(sin_o, tmpf, AF.Sin, r=[tag + "tf"], w=[tag + "os"])
            stt("dve", tmpf, tmpf, -1.0, tmpf, ALU.mult, ALU.max, r=[tag + "tf", tag + "os"], w=[tag + "tf"])
            ts("dve", tmpf, tmpf, -1.0, float(np.pi / 2), ALU.mult, ALU.add, r=[tag + "tf"], w=[tag + "tf"])
            act(cos_o, tmpf, AF.Sin, r=[tag + "tf"], w=[tag + "oc"])

        P = [st.enter_context(nc.psum_tensor("P%d" % i, [128, 512], F32)) for i in range(7)]
        PT = st.enter_context(nc.psum_tensor("PTb", [128, 1024], BF16))
        PN = ["P%d" % i for i in range(7)]
        id_bf = sbuf(st, "id_bf", [128, 128], BF16)
        tri_bf = sbuf(st, "tri_bf", [128, 128], BF16)
        blk1_bf = sbuf(st, "blk1_bf", [128, 128], BF16)
        ones_bf = sbuf(st, "ones_bf", [128, 128], BF16)
        iota_c = sbuf(st, "iota_c", [128, 1])
        iota_r = sbuf(st, "iota_r", [128, 128])
        g1c = sbuf(st, "g1c", [128, 8]); bgc = sbuf(st, "bgc", [128, 16])
        qgc = sbuf(st, "qgc", [128, 1]); kgc = sbuf(st, "kgc", [128, 1])
        g2r = sbuf(st, "g2r", [128, D])
        hbuf = sbuf(st, "hbuf", [128, NB, D])

        dma("pool", id_bf[:], ident, w=["c"], key="setup_pool")
        dma("pool", tri_bf[:], tri, w=["c"], key="setup_pool")
        dma("pool", blk1_bf[:], blk1, w=["c"], key="setup_pool")
        dma("sp", iota_c[:], iota_col, w=["c"], key="setup_sp")
        dma("sp", iota_r[:], iota_row, w=["c"], key="setup_sp")
        dma("sp", g1c[:], g1col, w=["c"], key="setup_sp")
        dma("sp", bgc[:], bgcol, w=["c"], key="setup_sp")
        dma("sp", qgc[:], qgcol, w=["c"], key="setup_sp")
        dma("sp", kgc[:], kgcol, w=["c"], key="setup_sp")
        dma("sp", g2r[:], g2row, w=["c"], key="setup_sp")
        memset("dve", ones_bf[:], 1.0, w=["ones"])
        S.barrier()

        w_in_v = w_in.rearrange("(k p) n -> p k n", p=128)

        uvb = nc.dram_tensor("uvb", [NEXP, 2 * D], BF16, kind="Internal").ap()
        CV = 512
        S.nobarrier.add("cvt")
        cvt_list = []
        for r0 in range(0, NEXP, CV):
            cvt_list.append((uvb[r0:r0 + CV, 0:D], peer_u[r0:r0 + CV, :], "uvbu%d" % r0))
            cvt_list.append((uvb[r0:r0 + CV, D:2 * D], peer_v[r0:r0 + CV, :], "uvbv%d" % r0))

        def issue_cvt(n):
            for _ in range(n):
                if cvt_list:
                    o_, i_, nm_ = cvt_list.pop(0)
                    dma("pool", o_, i_, w=[nm_], key="cvt")
        UVB_ALL = ["uvbu%d" % r0 for r0 in range(0, NEXP, CV)] + ["uvbv%d" % r0 for r0 in range(0, NEXP, CV)]

        with contextlib.ExitStack() as s1:
            xnT = sbuf(s1, "xnT", [128, 8, T], BF16)
            mix = sbuf(s1, "mix", [128, 8, T], BF16)

            ss_a = sbuf(s1, "ss_a", [128, NB])

            def emit_1a():
                mflat = mix[:].rearrange("p k t -> p (k t)")
                xb = [mflat[:, i * 2048:(i + 1) * 2048].bitcast(F32) for i in range(2)]
                xnb = [mflat[:, 4096 + i * 1024:4096 + (i + 1) * 1024] for i in range(2)]
                junk = mflat[:, 6144:8192].bitcast(F32)
                for b in range(NB):
                    i = b % 2
                    dma("sp", xb[i], x[b * 128:(b + 1) * 128, :], w=["xb%d" % i], key="xb%d" % i)
                    act(junk, xb[i], AF.Square, r=["xb%d" % i], w=["junk_a", "ss%d" % b], accum=ss_a[:, b:b + 1])
                    rsqrt_chain(ss_a[:, b:b + 1], ss_a[:, b:b + 1], 1.0 / D, r=["ss%d" % b], w=["ss%d" % b], tag="a")
                    ts("dve", xnb[i], xb[i], ss_a[:, b:b + 1], None, ALU.mult, r=["xb%d" % i, "ss%d" % b], w=["xnb%d" % i])
                    for k in range(8):
                        tr(PT[:, k * 128:(k + 1) * 128], xnb[i][:, k * 128:(k + 1) * 128], r=["xnb%d" % i], w=["PT"])
                    tt("dve", xnT[:, :, b * 128:(b + 1) * 128], PT[:].rearrange("p (k t) -> p k t", k=8),
                       g1c[:].unsqueeze(2).to_broadcast([128, 8, 128]), ALU.mult, r=["PT"], w=["xnT%d" % b])

            S.defer_begin()
            emit_1a()
            ops_1a = S.defer_end()
            XN_ALL = ["xnT%d" % b for b in range(NB)]

            def xn_chunk(c):
                return ["xnT%d" % b for b in range(4 * c, 4 * c + 4)]


            def harena(b0, nb):
                return hbuf[:, b0:b0 + nb, :].rearrange("p a b -> p (a b)")

            with contextlib.ExitStack() as sb_:
                uT = harena(8, 4).bitcast(BF16).rearrange("p (j t) -> p j t", j=4)
                yg = harena(12, 4).bitcast(BF16).rearrange("p (j t) -> p j t", j=4)
                Rre = harena(0, 2); Rim = harena(2, 2)
                Pre = harena(4, 2).rearrange("p (a t) -> p a t", a=16); Pim = harena(6, 2).rearrange("p (a t) -> p a t", a=16)
                with contextlib.ExitStack() as sc:
                    w_u = sbuf(sc, "w_u", [128, 8, 512], BF16)
                    Bbd = sbuf(sc, "Bbd", [128, 4, 2, 512], BF16)
                    Cre = sbuf(sc, "Cre", [128, 16, 128], BF16); Cim = sbuf(sc, "Cim", [128, 16, 128], BF16)
                    Dd = sbuf(sc, "Dd", [128, 4, 128], BF16)
                    a128 = sbuf(sc, "a128", [128, 2, 16])
                    car = sbuf(sc, "car", [128, 2, 16])
                    wload(w_u, w_in_v[:, :, 0:512], 8, ["w_u"], "w_u")
                    dma("pool", Cre[:].rearrange("p a b -> p (a b)"), cpad_re, w=["Cre"], key="Cre")
                    dma("pool", Cim[:].rearrange("p a b -> p (a b)"), cpad_im, w=["Cim"], key="Cim")
                    dma("pool", Dd[:].rearrange("p a b -> p (a b)"), ddiag, w=["Dd"], key="Dd")
                    ts("pool", Cim[:], Cim[:], -1.0, None, ALU.mult, r=["Cim"], w=["Cim"])
                    memset("pool", car[:], 0.0, w=["car0", "car1", "car2", "car3"])
                    with contextlib.ExitStack() as sx:
                        X1, X2, X3, X4 = [sbuf(sx, "X%d" % i, [128, 2048])[:] for i in range(4)]
                        W1, W2, W3, W4 = [harena(8 + 2 * i, 2) for i in range(4)]
                        A_, B_, C_, D_ = Rre, Rim, harena(4, 2), harena(6, 2)
                        TI = A_.bitcast(I32)
                        colp = sbuf(sx, "colp", [128, 6, 16])
                        nsc = sbuf(sx, "nsc", [128, 1])
                        smt = sbuf(sx, "smt", [128, 4, 16]); smi = sbuf(sx, "smi", [128, 16], I32)
                        Z = ["Z"]
                        S.interleave = (ops_1a, 2)
                        dma("sp", W1, ls_row, w=Z, key="sx0"); dma("sp", W2, lr_row, w=Z, key="sx1"); dma("sp", W3, li_row, w=Z, key="sx2")
                        dma("sp", colp[:, 0, :], ls_col, w=Z, key="sx3"); dma("sp", colp[:, 1, :], lr_col, w=Z, key="sx4"); dma("sp", colp[:, 2, :], li_col, w=Z, key="sx5")
                        act(W1, W1, AF.Exp, r=Z, w=Z)
                        tt("dve", W4, W2, W1, ALU.mult, r=Z, w=Z)
                        tt("dve", X1, W3, W1, ALU.mult, r=Z, w=Z)
                        sincos(X2, X3, X1, X4, TI, Z, "Z")
                        act(W1, W4, AF.Exp, r=Z, w=Z)
                        tt("dve", X3, X3, W1, ALU.mult, r=Z, w=Z)
                        tt("dve", X2, X2, W1, ALU.mult, r=Z, w=Z)
                        ts("dve", X3, X3, -1.0, None, ALU.add, r=Z, w=Z)
                        tt("dve", B_, W2, W2, ALU.mult, r=Z, w=Z)
                        tt("dve", C_, W3, W3, ALU.mult, r=Z, w=Z)
                        tt("dve", B_, B_, C_, ALU.add, r=Z, w=Z)
                        recip(B_, B_, r=Z, w=Z)
                        tt("dve", C_, X3, W2, ALU.mult, r=Z, w=Z)
                        tt("dve", D_, X2, W3, ALU.mult, r=Z, w=Z)
                        tt("dve", C_, C_, D_, ALU.add, r=Z, w=Z)
                        tt("dve", C_, C_, B_, ALU.mult, r=Z, w=Z)
                        tt("dve", D_, X2, W2, ALU.mult, r=Z, w=Z)
                        tt("dve", W1, X3, W3, ALU.mult, r=Z, w=Z)
                        tt("dve", D_, D_, W1, ALU.subtract, r=Z, w=Z)
                        tt("dve", D_, D_, B_, ALU.mult, r=Z, w=Z)
                        dma("sp", X2, bT_re, r=Z, w=Z, key="sx6"); dma("sp", X3, bT_im, r=Z, w=Z, key="sx7")
                        v4 = lambda ap: ap.rearrange("p (j n) -> p j n", j=4)
                        tt("dve", W1, X2, C_, ALU.mult, r=Z, w=Z)
                        tt("dve", B_, X3, D_, ALU.mult, r=Z, w=Z)
                        tt("dve", Bbd[:, :, 0, :], v4(W1), v4(B_), ALU.subtract, r=Z, w=Z + ["Bbd"])
                        tt("dve", W1, X2, D_, ALU.mult, r=Z, w=Z)
                        tt("dve", B_, X3, C_, ALU.mult, r=Z, w=Z)
                        tt("dve", Bbd[:, :, 1, :], v4(W1), v4(B_), ALU.add, r=Z, w=Z + ["Bbd"])
                        ts("dve", nsc[:], iota_c[:], -1.0, None, ALU.mult, r=Z, w=Z)
                        ts("dve", B_, X1, iota_c[:, 0:1], None, ALU.mult, r=Z, w=Z)
                        sincos(X2, X3, B_, X4, TI, Z, "Z")
                        act(W1, W4, AF.Exp, r=Z, w=Z, scale=nsc[:, 0:1])
                        tt("dve", A_, X3, W1, ALU.mult, r=Z, w=Z)
                        tt("dve", B_, X2, W1, ALU.mult, r=Z, w=Z)
                        ts("dve", B_, B_, -1.0, None, ALU.mult, r=Z, w=Z)
                        act(colp[:, 0, :], colp[:, 0, :], AF.Exp, r=Z, w=Z)
                        tt("dve", colp[:, 3, :], colp[:, 1, :], colp[:, 0, :], ALU.mult, r=Z, w=Z)
                        tt("dve", colp[:, 4, :], colp[:, 2, :], colp[:, 0, :], ALU.mult, r=Z, w=Z)
                        ts("dve", smt[:, 0, :], colp[:, 4, :], 128.0, None, ALU.mult, r=Z, w=Z)
                        sincos(smt[:, 1, :], smt[:, 2, :], smt[:, 0, :], smt[:, 3, :], smi[:], Z, "Z")
                        act(smt[:, 0, :], colp[:, 3, :], AF.Exp, r=Z, w=Z, scale=128.0)
                        tt("dve", a128[:, 0, :], smt[:, 2, :], smt[:, 0, :], ALU.mult, r=Z, w=Z)
                        tt("dve", a128[:, 1, :], smt[:, 1, :], smt[:, 0, :], ALU.mult, r=Z, w=Z)
                        x3 = lambda ap: ap.rearrange("p (a t) -> p a t", a=16)
                        ir_b = iota_r[:].unsqueeze(1).to_broadcast([128, 16, 128])
                        tt("dve", x3(X2), ir_b, colp[:, 4, :].unsqueeze(2).to_broadcast([128, 16, 128]), ALU.mult, r=Z, w=Z)
                        sincos(X3, W2, X2, X4, W3.bitcast(I32), Z, "Z")
                        tt("dve", x3(X1), ir_b, colp[:, 3, :].unsqueeze(2).to_broadcast([128, 16, 128]), ALU.mult, r=Z, w=Z)
                        act(X1, X1, AF.Exp, r=Z, w=Z)
                        tt("dve", C_, W2, X1, ALU.mult, r=Z, w=Z)
                        tt("dve", D_, X3, X1, ALU.mult, r=Z, w=Z)
                        S.interleave = None
                        S.flush(ops_1a, len(ops_1a))
                        if dbg:
                            dma("pool", dbgo["xnT"], xnT[:].rearrange("p k t -> p (k t)"), r=["xnT%d" % b for b in range(NB)], key="dbg")
                        S.barrier()

                    for j in range(4):
                        for c in range(NCH):
                            pb = P[(j * 4 + c) % 2]; pn = PN[(j * 4 + c) % 2]
                            for k in range(8):
                                mm(pb[:], w_u[:, k, j * 128:(j + 1) * 128], xnT[:, k, c * 512:(c + 1) * 512], k == 0, k == 7,
                                   r=["w_u"] + xn_chunk(c), w=[pn])
                            cp("act", uT[:, j, c * 512:(c + 1) * 512], pb[:], r=[pn], w=["uT%d_%d" % (j, c)])

                    with contextlib.ExitStack() as sw:
                        tq = [[sbuf(sw, "tq%d_%d" % (pp, i), [128, 512]) for i in range(4)] for pp in range(2)]
                        Bt = [sbuf(sw, "Bt%d" % pp, [128, 2, 512], BF16) for pp in range(2)]
                        tp = [sbuf(sw, "tp%d" % i, [128, 4, 128]) for i in range(4)]
                        Hh = [sbuf(sw, "Hh%d" % i_, [128, 2, 4, 128], BF16) for i_ in range(2)]
                        gs = sbuf(sw, "gs", [128, 2, 4]); gt = sbuf(sw, "gt", [128, 4, 4])
                        Gc = sbuf(sw, "Gc", [128, 2, 4, 128])
                        its = [(b, j) for b in range(NB) for j in range(4)]

                        def stageA(it):
                            b, j = its[it]; pp = it % 2
                            issue_cvt(1)
                            bre, bim = P[2 * pp], P[2 * pp + 1]; nre, nim = PN[2 * pp], PN[2 * pp + 1]
                            blk = slice(b * 128, (b + 1) * 128)
                            ur = ["uT%d_%d" % (j, b // 4), "Bbd"]
                            mm(bre[:], uT[:, j, blk], Bbd[:, j, 0, :], True, True, r=ur, w=[nre])
                            mm(bim[:], uT[:, j, blk], Bbd[:, j, 1, :], True, True, r=ur, w=[nim])
                            rs_ = slice(j * 512, (j + 1) * 512)
                            t = tq[pp]
                            tt("dve", t[0][:], bre[:], Rre[:, rs_], ALU.mult, r=[nre], w=["tq%d_0" % pp])
                            tt("dve", t[1][:], bim[:], Rim[:, rs_], ALU.mult, r=[nim], w=["tq%d_1" % pp])
                            tt("dve", t[2][:], bre[:], Rim[:, rs_], ALU.mult, r=[nre], w=["tq%d_2" % pp])
                            tt("dve", t[3][:], bim[:], Rre[:, rs_], ALU.mult, r=[nim], w=["tq%d_3" % pp])
                            tt("pool", Bt[pp][:, 0, :], t[0][:], t[1][:], ALU.subtract, r=["tq%d_0" % pp, "tq%d_1" % pp], w=["Bt%d_0" % pp])
                            tt("pool", Bt[pp][:, 1, :], t[2][:], t[3][:], ALU.add, r=["tq%d_2" % pp, "tq%d_3" % pp], w=["Bt%d_1" % pp])

                        def stageB(it):
                            b, j = its[it]; pp = it % 2
                            G = [P[4], P[5]]; GN = [PN[4], PN[5]]
                            for ri in range(2):
                                for q in range(4):
                                    mm(G[ri][:, q * 128:(q + 1) * 128], Bt[pp][:, ri, q * 128:(q + 1) * 128], tri_bf[:], True, True,
                                       r=["Bt%d_%d" % (pp, ri)], w=[GN[ri]])
                            cn = "car%d" % j
                            for ri in range(2):
                                for q in range(4):
                                    Tq = 4 * j + q
                                    act(Gc[:, ri, q, :], G[ri][:, q * 128:(q + 1) * 128], AF.Identity, r=[GN[ri], cn], w=["Gc%d" % ri], bias=car[:, ri, Tq:Tq + 1])
                            f3 = lambda ap: ap.rearrange("p q t -> p (q t)")
                            js = slice(4 * j, 4 * j + 4)
                            tt("dve", f3(tp[0][:]), f3(Gc[:, 0]), f3(Pre[:, js, :]), ALU.mult, r=["Gc0"], w=["tp0"])
                            tt("dve", f3(tp[1][:]), f3(Gc[:, 1]), f3(Pim[:, js, :]), ALU.mult, r=["Gc1"], w=["tp1"])
                            tt("dve", f3(tp[2][:]), f3(Gc[:, 0]), f3(Pim[:, js, :]), ALU.mult, r=["Gc0"], w=["tp2"])
                            tt("dve", f3(tp[3][:]), f3(Gc[:, 1]), f3(Pre[:, js, :]), ALU.mult, r=["Gc1"], w=["tp3"])
                            for ri in range(2):
                                cp("pool", gs[:, ri, :], Gc[:, ri, :, 127], r=["Gc%d" % ri], w=["gs%d" % ri])
                            tt("pool", gt[:, 0, :], gs[:, 0, :], a128[:, 0, js], ALU.mult, r=["gs0"], w=["gt0"])
                            tt("pool", gt[:, 1, :], gs[:, 1, :], a128[:, 1, js], ALU.mult, r=["gs1"], w=["gt1"])
                            tt("pool", gt[:, 2, :], gs[:, 0, :], a128[:, 1, js], ALU.mult, r=["gs0"], w=["gt2"])
                            tt("pool", gt[:, 3, :], gs[:, 1, :], a128[:, 0, js], ALU.mult, r=["gs1"], w=["gt3"])
                            tt("pool", car[:, 0, js], gt[:, 0, :], gt[:, 1, :], ALU.subtract, r=["gt0", "gt1"], w=[cn])
                            tt("pool", car[:, 1, js], gt[:, 2, :], gt[:, 3, :], ALU.add, r=["gt2", "gt3"], w=[cn])
                            f2 = lambda ap: ap.rearrange("p q t -> p (q t)")
                            tt("dve", f2(Hh[pp][:, 0]), f2(tp[0][:]), f2(tp[1][:]), ALU.subtract, r=["tp0", "tp1"], w=["Hh%d_0" % pp])
                            tt("dve", f2(Hh[pp][:, 1]), f2(tp[2][:]), f2(tp[3][:]), ALU.add, r=["tp2", "tp3"], w=["Hh%d_1" % pp])

                        def stageC(it):
                            b, j = its[it]; pp = it % 2
                            blk = slice(b * 128, (b + 1) * 128)
                            Y = P[6]
                            for q in range(4):
                                Tq = 4 * j + q
                                mm(Y[:, 0:128], Cre[:, Tq, :], Hh[pp][:, 0, q, :], q == 0, False, r=["Cre", "Hh%d_0" % pp], w=[PN[6]])
                                mm(Y[:, 0:128], Cim[:, Tq, :], Hh[pp][:, 1, q, :], False, False, r=["Cim", "Hh%d_1" % pp], w=[PN[6]])
                            mm(Y[:, 0:128], Dd[:, j, :], uT[:, j, blk], False, True, r=["Dd", "uT%d_%d" % (j, b // 4)], w=[PN[6]])
                            act(yg[:, j, blk], Y[:, 0:128], AF.Gelu, r=[PN[6]], w=["yg%d_%d" % (j, b)])

                        stageA(0)
                        for it in range(len(its)):
                            if it + 1 < len(its):
                                stageA(it + 1)
                            stageB(it)
                            if it >= 1:
                                stageC(it - 1)
                        stageC(len(its) - 1)
                        issue_cvt(len(cvt_list))
                        S.barrier()
                    S.emit()
                if stop_after == "scan":
                    dma("pool", dbgo["yg"], yg.rearrange("p j t -> p (j t)"), key="dbg")
                    S.emit()
                    return nc

                with contextlib.ExitStack() as sg:
                    w_gl = sbuf(sg, "w_gl", [128, 4, 1024], BF16)
                    s5o = sbuf(sg, "s5o", [128, 4, T], BF16)
                    sig = [sbuf(sg, "sig%d" % i, [128, 512]) for i in range(2)]
                    w_gs = sbuf(sg, "w_gs", [128, 8, 1024], BF16)
                    w_sb = sbuf(sg, "w_sb", [128, 4, 1024], BF16)
                    wload(w_gl, w_glu.rearrange("(k p) n -> p k n", p=128), 4, ["w_gl"], "w_gl")
                    wload(w_gs, w_in_v[:, :, 2048:3072], 8, ["w_gs"], "w_gs")
                    wload(w_sb, w_s5b.rearrange("(k p) n -> p k n", p=128), 4, ["w_sb"], "w_sb")
                    for n in range(4):
                        for c in range(NCH):
                            i = (n * 4 + c) % 2
                            pa, pb = P[2 * i], P[2 * i + 1]; na, nb_ = PN[2 * i], PN[2 * i + 1]
                            cs_ = slice(c * 512, (c + 1) * 512)
                            for k in range(4):
                                mm(pa[:], w_gl[:, k, n * 128:(n + 1) * 128], yg[:, k, cs_], k == 0, k == 3, r=["w_gl"], w=[na])
                            for k in range(4):
                                mm(pb[:], w_gl[:, k, 512 + n * 128:512 + (n + 1) * 128], yg[:, k, cs_], k == 0, k == 3, r=["w_gl"], w=[nb_])
                            act(sig[i][:], pb[:], AF.Sigmoid, r=[nb_], w=["sig%d" % i])
                            tt("dve", s5o[:, n, cs_], pa[:], sig[i][:], ALU.mult, r=[na, "sig%d" % i], w=["s5o_%d" % c])
                    for n in range(8):
                        for c in range(NCH):
                            i = (n * 4 + c) % 2
                            pg, py = P[2 * i], P[2 * i + 1]; ng, ny = PN[2 * i], PN[2 * i + 1]
                            cs_ = slice(c * 512, (c + 1) * 512)
                            for k in range(8):
                                mm(pg[:], w_gs[:, k, n * 128:(n + 1) * 128], xnT[:, k, cs_], k == 0, k == 7, r=["w_gs"], w=[ng])
                            for k in range(4):
                                mm(py[:], w_sb[:, k, n * 128:(n + 1) * 128], s5o[:, k, cs_], k == 0, k == 3, r=["w_sb", "s5o_%d" % c], w=[ny])
                            act(sig[i][:], pg[:], AF.Sigmoid, r=[ng], w=["sig%d" % i], bias=bgc[:, n:n + 1])
                            tt("dve", mix[:, n, cs_], py[:], sig[i][:], ALU.mult, r=[ny, "sig%d" % i], w=["mix%d_%d" % (n, c)])
                    if dbg:
                        dma("pool", dbgo["yg"], s5o[:].rearrange("p j t -> p (j t)"), r=["s5o_%d" % c for c in range(NCH)], key="dbg")
                        if stop_after == "1b":
                            dma("pool", dbgo["mix"], mix[:].rearrange("p j t -> p (j t)"), r=["mix%d_%d" % (n, c) for n in range(8) for c in range(NCH)], key="dbg")
                    S.emit()
            if stop_after == "1b":
                return nc

            aoT = harena(0, 8).bitcast(BF16).rearrange("p (j t) -> p j t", j=8)
            with contextlib.ExitStack() as sc_:
                qbd = harena(8, 4).bitcast(BF16).rearrange("p (j b c) -> p j b c", j=2, b=NB)
                kT_a = harena(12, 1).bitcast(BF16)
                v_a = harena(13, 1).bitcast(BF16).rearrange("p (b n) -> p b n", b=NB)
                BMa = [harena(14 + i, 1) for i in range(2)]
                maskt = sbuf(sc_, "maskt", [128, 1024])[:]
                memset("pool", harena(8, 4).bitcast(BF16), 0.0, w=["qbd"])
                wgrp = [sbuf(sc_, "wgrp%d" % i, [128, 8, 512], BF16) for i in range(2)]
                sq = [sbuf(sc_, "sq%d" % i, [128, 512], BF16) for i in range(2)]
                rs = [sbuf(sc_, "rs%d" % i, [128, 512]) for i in range(2)]
                tmpS = [sbuf(sc_, "tmpS%d" % i, [128, 2, 512]) for i in range(2)]
                PTb = [sbuf(sc_, "PTb_s%d" % i, [128, 2, 512], BF16) for i in range(2)]
                rc = [sbuf(sc_, "rc%d" % i, [128, 512]) for i in range(2)]
                PTf = PT[:].bitcast(F32)
                sk_sb = sbuf(sc_, "sk_sb", [128, 16])
                esink = sbuf(sc_, "esink", [128, 16 * 128])
                qgs = sbuf(sc_, "qgs", [128, 1])
                epsc = sbuf(sc_, "epsc", [128, 1])
                memset("pool", epsc[:], EPS, w=["epsc"])
                dma("sp", maskt, mask_t, w=["maskt"], key="maskt")
                dma("sp", sk_sb[:], sinks, w=["sk_sb"], key="sk_sb")
                act(sk_sb[:], sk_sb[:], AF.Exp, r=["sk_sb"], w=["sk_sb"])
                cp("dve", esink[:].rearrange("p (h q) -> p h q", h=16), sk_sb[:].unsqueeze(2).to_broadcast([128, 16, 128]), r=["sk_sb"], w=["esink"])
                ts("dve", qgs[:], qgc[:], 0.125, None, ALU.mult, w=["qgs"])

                def load_grp(a):
                    wg = wgrp[a % 2]; nm = "wgrp%d" % (a % 2)
                    dma("pool", wg[:, :, 0:256], w_in_v[:, :, 512 + a * 256:512 + (a + 1) * 256], w=[nm], key=nm)
                    for dup in range(2):
                        dma("pool", wg[:, :, 256 + dup * 64:320 + dup * 64], w_in_v[:, :, 1536 + a * 64:1536 + (a + 1) * 64], w=[nm], key=nm)
                        dma("pool", wg[:, :, 384 + dup * 64:448 + dup * 64], w_in_v[:, :, 1792 + a * 64:1792 + (a + 1) * 64], w=[nm], key=nm)
                    bn = "BMa%d" % (a % 2)
                    dma("sp", BMa[a % 2], bias_t[:, a, :], w=[bn], key=bn)
                    tt("pool", BMa[a % 2], BMa[a % 2], maskt, ALU.add, r=[bn, "maskt"], w=[bn])

                load_grp(0)
                for a in range(4):
                    if a + 1 < 4:
                        load_grp(a + 1)
                    wg = wgrp[a % 2]; nm = "wgrp%d" % (a % 2); bn = "BMa%d" % (a % 2)
                    for (dst, col0, gcol, tag) in ((0, 0, qgs, "q"), (1, 128, qgs, "q"), (2, 256, kgc, "k")):
                        for c in range(NCH):
                            cs_ = slice(c * 512, (c + 1) * 512)
                            pq = P[c % 2]; nq = PN[c % 2]
                            for k in range(8):
                                mm(pq[:], wg[:, k, col0:col0 + 128], xnT[:, k, cs_], k == 0, k == 7, r=[nm], w=[nq])
                            ci = c % 2
                            act(sq[ci][:], pq[:], AF.Square, r=[nq], w=["sq%d" % ci])
                            mm(P[2 + ci][:], blk1_bf[:], sq[ci][:], True, True, r=["sq%d" % ci], w=[PN[2 + ci]])
                            act(rs[ci][:], P[2 + ci][:], AF.Ln, r=[PN[2 + ci], "epsc"], w=["rs%d" % ci], scale=1.0 / 64, bias=epsc[:, 0:1])
                            act(rs[ci][:], rs[ci][:], AF.Exp, r=["rs%d" % ci], w=["rs%d" % ci], scale=-0.5)
                            if dst < 2:
                                for half in range(2):
                                    hs = slice(half * 64, half * 64 + 64)
                                    v4b = lambda ap: ap.rearrange("p (b q) -> p b q", b=4)
                                    stt("dve", qbd[hs, dst, 4 * c:4 * c + 4, half * 128:(half + 1) * 128], v4b(pq[hs, :]), gcol[hs, 0:1], v4b(rs[ci][hs, :]),
                                        ALU.mult, ALU.mult, r=[nq, "rs%d" % ci, "qgs", "qbd"], w=["qk%d_%d" % (dst, c)])
                            else:
                                stt("dve", kT_a[:, cs_], pq[:], gcol[:, 0:1], rs[ci][:], ALU.mult, ALU.mult, r=[nq, "rs%d" % ci, "qgs"], w=["qk%d_%d" % (dst, c)])
                    for b4 in range(4):
                        pv = P[6]
                        for bb in range(4):
                            blk = b4 * 4 + bb
                            for k in range(8):
                                mm(pv[:, bb * 128:(bb + 1) * 128], xnT[:, k, blk * 128:(blk + 1) * 128], wg[:, k, 384:512], k == 0, k == 7, r=[nm], w=[PN[6]])
                        cp("act", v_a[:, b4 * 4:(b4 + 1) * 4, :], pv[:].rearrange("p (b n) -> p b n", b=4), r=[PN[6]], w=["v_a%d" % b4])
                    SBK = {(0, 0): 0, (0, 1): 1, (1, 0): 4, (1, 1): 5}

                    def att_stage1(n):
                        p = n % 2
                        nbk = slice(n * 128, (n + 1) * 128)
                        pbk = slice((n - 1) * 128, n * 128)
                        qr = ["qk0_%d" % (n // 4), "qk1_%d" % (n // 4)]
                        kbs = [1] if n == 0 else [0, 1]
                        for kb in kbs:
                            bk = SBK[(p, kb)]
                            kblk = nbk if kb == 1 else pbk
                            for j2 in range(2):
                                mm(P[bk][:, j2 * 256:(j2 + 1) * 256], kT_a[:, kblk], qbd[:, j2, n, :], True, True,
                                   r=qr + ["qk2_%d" % ((n - 1 + kb) // 4)], w=[PN[bk]])
                        for kb in kbs:
                            bk = SBK[(p, kb)]
                            tt("dve", tmpS[p][:, kb, :], P[bk][:], BMa[a % 2][:, kb * 512:(kb + 1) * 512], ALU.add, r=[PN[bk], bn], w=["tmpS%d_%d" % (p, kb)])
                        k0 = kbs[0]
                        act(PTb[p][:, k0:2, :], tmpS[p][:, k0:2, :], AF.Exp, r=["tmpS%d_%d" % (p, kb) for kb in kbs], w=["PTb%d_0" % p, "PTb%d_1" % p])

                    def att_stage2(n):
                        p = n % 2
                        nbk = slice(n * 128, (n + 1) * 128)
                        kbs = [1] if n == 0 else [0, 1]
                        Nn, nn_ = (P[6][:], PN[6]) if p == 0 else (PTf, "PT")
                        Dn, dn_ = (P[2][:], PN[2]) if p == 0 else (P[3][:], PN[3])
                        rcp, rn_ = rc[p], "rc%d" % p
                        for idx, kb in enumerate(kbs):
                            blk = n - 1 + kb
                            mm(Nn, v_a[:, blk, :], PTb[p][:, kb, :], idx == 0, idx == len(kbs) - 1, r=["v_a%d" % (blk // 4), "PTb%d_0" % p, "PTb%d_1" % p], w=[nn_])
                            mm(Dn, ones_bf[:], PTb[p][:, kb, :], idx == 0, idx == len(kbs) - 1, r=["PTb%d_0" % p, "PTb%d_1" % p], w=[dn_])
                        tt("dve", rcp[:], Dn, esink[:, a * 512:(a + 1) * 512], ALU.add, r=[dn_, "esink"], w=[rn_])
                        act(rcp[:], rcp[:], AF.Ln, r=[rn_], w=[rn_])
                        act(rcp[:], rcp[:], AF.Exp, r=[rn_], w=[rn_], scale=-1.0)
                        for j2 in range(2):
                            for half in range(2):
                                hs = slice(half * 64, half * 64 + 64)
                                cc = slice((2 * j2 + half) * 128, (2 * j2 + half + 1) * 128)
                                tt("dve", aoT[hs, 2 * a + j2, nbk], Nn[hs, cc], rcp[hs, cc], ALU.mult, r=[nn_, rn_], w=["ao%d" % (2 * a + j2)])

                    att_stage1(0)
                    for n in range(NB):
                        if n + 1 < NB:
                            att_stage1(n + 1)
                        att_stage2(n)
                if dbg:
                    dma("pool", dbgo["ao"], aoT.rearrange("p j t -> p (j t)"), r=["ao%d" % i for i in range(8)], key="dbg")
                S.emit()
            if stop_after == "1c":
                return nc

            with contextlib.ExitStack() as sd:
                w_ga = sbuf(sd, "w_ga", [128, 8, 1024], BF16)
                w_at = sbuf(sd, "w_at", [128, 8, 1024], BF16)
                w_o = sbuf(sd, "w_o", [128, 8, 1024], BF16)
                xb = [sbuf(sd, "xbe%d" % i, [128, D]) for i in range(2)]
                sig = [sbuf(sd, "sigd%d" % i, [128, 512]) for i in range(2)]
                tm = [sbuf(sd, "tmd%d" % i, [128, 512]) for i in range(2)]
                wload(w_ga, w_in_v[:, :, 3072:4096], 8, ["w_ga"], "w_ga")
                wload(w_at, w_attn.rearrange("(k p) n -> p k n", p=128), 8, ["w_at"], "w_at")
                wload(w_o, w_out.rearrange("(k p) n -> p k n", p=128), 8, ["w_o"], "w_o")
                for i in range(2):
                    dma("sp", xb[i][:], x[i * 128:(i + 1) * 128, :], w=["xbe%d" % i], key="xbe%d" % i)
                for n in range(8):
                    for c in range(NCH):
                        i = (n * 4 + c) % 2
                        pg, py = P[2 * i], P[2 * i + 1]; ng, ny = PN[2 * i], PN[2 * i + 1]
                        cs_ = slice(c * 512, (c + 1) * 512)
                        for k in range(8):
                            mm(pg[:], w_ga[:, k, n * 128:(n + 1) * 128], xnT[:, k, cs_], k == 0, k == 7, r=["w_ga"], w=[ng])
                        for k in range(8):
                            mm(py[:], w_at[:, k, n * 128:(n + 1) * 128], aoT[:, k, cs_], k == 0, k == 7, r=["w_at"], w=[ny])
                        act(sig[i][:], pg[:], AF.Sigmoid, r=[ng], w=["sigd%d" % i], bias=bgc[:, 8 + n:9 + n])
                        tt("dve", tm[i][:], py[:], sig[i][:], ALU.mult, r=[ny, "sigd%d" % i], w=["tmd%d" % i])
                        tt("pool", mix[:, n, cs_], tm[i][:], mix[:, n, cs_], ALU.add, r=["tmd%d" % i, "mix%d_%d" % (n, c)], w=["mix%d_%d" % (n, c)])
                if dbg:
                    dma("pool", dbgo["mix"], mix[:].rearrange("p j t -> p (j t)"), r=["mix%d_%d" % (n, c) for n in range(8) for c in range(NCH)], key="dbg")
                S.barrier()
                for blk in range(NB):
                    i = blk % 2
                    bs_ = slice(blk * 128, (blk + 1) * 128)
                    if blk >= 2:
                        dma("sp", xb[i][:], x[bs_, :], w=["xbe%d" % i], key="xbe%d" % i)
                    for half in range(2):
                        ph = P[(2 * blk + half) % 4]; nh = PN[(2 * blk + half) % 4]
                        hc = slice(half * 512, (half + 1) * 512)
                        for k in range(8):
                            mm(ph[:], mix[:, k, bs_], w_o[:, k, hc], k == 0, k == 7, r=["w_o"], w=[nh])
                        tt("dve", hbuf[:, blk, hc], ph[:], xb[i][:, hc], ALU.add, r=[nh, "xbe%d" % i], w=["h%d" % blk])
                if dbg:
                    dma("sp", dbgo["h"], hbuf[:].rearrange("p b d -> p (b d)"), r=["h%d" % b for b in range(NB)], key="dbg")
                S.emit()
        if stop_after == "1d":
            return nc

        NG = 10
        LA = 5
        with contextlib.ExitStack() as s2:
            wq = sbuf(s2, "wq", [128, 8, 1024], BF16)
            sk = sbuf(s2, "sk", [128, 8, 256], BF16)
            xn2b = [sbuf(s2, "xn2b%d" % i, [128, D], BF16) for i in range(2)]
            xn2T = sbuf(s2, "xn2T", [128, 8, 128], BF16)
            qpT = sbuf(s2, "qpT", [128, 8, 128], BF16)
            s_sb = sbuf(s2, "s_sb", [128, 16, 128])
            work = sbuf(s2, "work", [128, 16, 128])
            vals = sbuf(s2, "vals", [128, 16, 16])
            idxu = sbuf(s2, "idxu", [128, 16, 16], U32)
            idxf = sbuf(s2, "idxf", [128, 16, 16])
            cand = sbuf(s2, "cand", [128, 8, 256])
            work2 = work[:].rearrange("p (h a) n -> p h (a n)", a=2)
            best = sbuf(s2, "best", [128, 8, 16])
            posu = sbuf(s2, "posu", [128, 8, 16], U32)
            posf = sbuf(s2, "posf", [128, 8, 16]); rf = sbuf(s2, "rf", [128, 8, 16]); cf = sbuf(s2, "cf", [128, 8, 16])
            ri_ = sbuf(s2, "ri_", [128, 8, 16], I32)
            oh = sbuf(s2, "oh", [128, 8, 16, 16])
            i1 = sbuf(s2, "i1", [128, 8, 16]); i2 = sbuf(s2, "i2", [128, 8, 16])
            eidf = sbuf(s2, "eidf", [128, 128])
            eid = [sbuf(s2, "eid%d" % i, [128, 128], I32) for i in range(2)]
            negm = sbuf(s2, "negm", [128, 8]); zz = sbuf(s2, "zz", [128, 8])
            ee = sbuf(s2, "ee", [128, 8, 16])
            gate = [sbuf(s2, "gate%d" % i, [128, 128]) for i in range(2)]
            a_sb = [sbuf(s2, "a_sb%d" % i, [128, 128]) for i in range(2)]
            ga = sbuf(s2, "ga", [128, 128]); wc = sbuf(s2, "wc", [128, 128])
            ss2 = sbuf(s2, "ss2", [128, NB])
            junk = oh[:].rearrange("p a b c -> p (a b c)")[:, 0:D]
            junkb = [sbuf(s2, "junkb%d" % i, [128, D], BF16) for i in range(3)]
            ug = [sbuf(s2, "ug%d" % i, [128, 2 * D], BF16) for i in range(NG)]
            dg = [sbuf(s2, "dg%d" % i, [128, 128], BF16) for i in range(4)]
            ob = [sbuf(s2, "ob%d" % i, [128, D]) for i in range(2)]
            wload(wq, w_pq.rearrange("(k p) n -> p k n", p=128), 8, ["wq"], "wq")
            dma("pool", sk[:].rearrange("p h n -> p (h n)"), skbd, w=["sk"], key="sk")

            def top16(dst_v, dst_i, lists, wk, nl, tag, lres):
                for l in range(nl):
                    S.add("dve", (lambda e, l=l: e.max(out=dst_v[:, l, 0:8], in_=lists[:, l, :])), r=[lres], w=[tag + "v0_%d" % l])
                for l in range(nl):
                    S.add("dve", (lambda e, l=l: e.max_index(out=dst_i[:, l, 0:8], in_max=dst_v[:, l, 0:8], in_values=lists[:, l, :])),
                          r=[lres, tag + "v0_%d" % l], w=[tag + "i0_%d" % l])
                wn = lambda l: ["wk_%d" % l] if nl == 16 else ["wk_%d" % (2 * l), "wk_%d" % (2 * l + 1)]
                for l in range(nl):
                    S.add("dve", (lambda e, l=l: e.match_replace(out=wk[:, l, :], in_to_replace=dst_v[:, l, 0:8], in_values=lists[:, l, :], imm_value=-1e30)),
                          r=[lres, tag + "v0_%d" % l], w=wn(l))
                for l in range(nl):
                    S.add("dve", (lambda e, l=l: e.max(out=dst_v[:, l, 8:16], in_=wk[:, l, :])), r=wn(l), w=[tag + "v1_%d" % l])
                for l in range(nl):
                    S.add("dve", (lambda e, l=l: e.max_index(out=dst_i[:, l, 8:16], in_max=dst_v[:, l, 8:16], in_values=wk[:, l, :])),
                          r=wn(l) + [tag + "v1_%d" % l], w=[tag + "i1_%d" % l])
                return [tag + "v0_%d" % l for l in range(nl)] + [tag + "v1_%d" % l for l in range(nl)], \
                       [tag + "i0_%d" % l for l in range(nl)] + [tag + "i1_%d" % l for l in range(nl)]

            def gather_by(idx_f, sel_f, dst, tag):
                io = iota_r[:, 0:16].unsqueeze(1).unsqueeze(1).to_broadcast([128, 8, 16, 16])
                tt("dve", oh[:], io, sel_f[:].unsqueeze(3).to_broadcast([128, 8, 16, 16]), ALU.is_equal, r=[tag + "sel"], w=["oh"])
                tt("pool", oh[:], oh[:], idx_f.unsqueeze(2).to_broadcast([128, 8, 16, 16]), ALU.mult, r=["oh", "idxf"], w=["oh"])
                S.add("dve", lambda e: e.tensor_reduce(out=dst[:], in_=oh[:], axis=AX.X, op=ALU.add), r=["oh"], w=[tag + "dst"])

            def front_end(b):
                pb = b % 2
                hb = hbuf[:, b, :]
                act(junk, hb, AF.Square, w=["oh", "ss2_%d" % b], accum=ss2[:, b:b + 1])
                rsqrt_chain(ss2[:, b:b + 1], ss2[:, b:b + 1], 1.0 / D, r=["ss2_%d" % b], w=["ss2_%d" % b], tag="p")
                stt("dve", xn2b[pb][:], hb, ss2[:, b:b + 1], g2r[:], ALU.mult, ALU.mult, r=["ss2_%d" % b], w=["xn2b%d" % pb])
                for k in range(8):
                    tr(PT[:, k * 128:(k + 1) * 128], xn2b[pb][:, k * 128:(k + 1) * 128], r=["xn2b%d" % pb], w=["PT"])
                cp("act", xn2T[:].rearrange("p k t -> p (k t)"), PT[:], r=["PT"], w=["xn2T"])
                for h in range(8):
                    pb_ = P[h // 4]
                    for k in range(8):
                        mm(pb_[:, (h % 4) * 128:(h % 4 + 1) * 128], wq[:, k, h * 128:(h + 1) * 128], xn2T[:, k, :], k == 0, k == 7, r=["wq", "xn2T"], w=[PN[h // 4]])
                for hh in range(2):
                    cp("act", qpT[:, hh * 4:(hh + 1) * 4, :].rearrange("p h t -> p (h t)"), P[hh][:], r=[PN[hh]], w=["qpT%d" % hh])
                for h in range(8):
                    bk = 2 + (h // 2) % 2
                    mm(P[bk][:, (h % 2) * 256:(h % 2 + 1) * 256], qpT[:, h, :], sk[:, h, :], True, True, r=["qpT%d" % (h // 4), "sk"], w=[PN[bk]])
                    if h % 2 == 1:
                        i = h // 2
                        cp("act", s_sb[:, 4 * i:4 * i + 4, :].rearrange("p l n -> p (l n)"), P[bk][:], r=[PN[bk]], w=["sL"])
                vr, ir = top16(vals, idxu, s_sb, work, 16, "s", "sL")
                cp("dve", idxf[:], idxu[:], r=ir, w=["idxf"])
                v3 = vals[:].rearrange("p (h k) r -> p h k r", k=2)
                tt("dve", cand[:].rearrange("p h (r c) -> p h r c", r=16), v3[:, :, 0, :].unsqueeze(3).to_broadcast([128, 8, 16, 16]),
                   v3[:, :, 1, :].unsqueeze(2).to_broadcast([128, 8, 16, 16]), ALU.add, r=vr, w=["cL"])
                vr2, ir2 = top16(best, posu, cand, work2, 8, "c", "cL")
                cp("dve", posf[:], posu[:], r=ir2, w=["posf"])
                ts("dve", rf[:], posf[:], -7.5, 1.0 / 16, ALU.add, ALU.mult, r=["posf"], w=["rf"])
                cp("dve", ri_[:], rf[:], r=["rf"], w=["ri_"])
                cp("dve", rf[:], ri_[:], r=["ri_"], w=["rsel"])
                stt("dve", cf[:], rf[:], -16.0, posf[:], ALU.mult, ALU.add, r=["rsel", "posf"], w=["csel"])
                i3 = idxf[:].rearrange("p (h k) r -> p h k r", k=2)
                gather_by(i3[:, :, 0, :], rf, i1, "r")
                gather_by(i3[:, :, 1, :], cf, i2, "c")
                stt("dve", eidf[:].rearrange("p (h j) -> p h j", h=8), i1[:], 128.0, i2[:], ALU.mult, ALU.add, r=["rdst", "cdst"], w=["eidf"])
                cp("dve", eid[pb][:], eidf[:], r=["eidf"], w=["eid%d" % pb])
                ts("dve", negm[:], best[:, :, 0], -1.0, None, ALU.mult, r=vr2, w=["negm"])
                for h in range(8):
                    act(ee[:, h, :], best[:, h, :], AF.Exp, r=vr2 + ["negm"], w=["ee%d" % h, "zz%d" % h], bias=negm[:, h:h + 1], accum=zz[:, h:h + 1], sticky=(h > 0))
                recip(zz[:], zz[:], r=["zz%d" % h for h in range(8)], w=["rz"])
                tt("dve", gate[pb][:].rearrange("p (h j) -> p h j", h=8), ee[:], zz[:].unsqueeze(2).to_broadcast([128, 8, 16]), ALU.mult,
                   r=["rz"] + ["ee%d" % h for h in range(8)], w=["gate%d" % pb])
                memset("pool", a_sb[pb][:], 0.0, w=["a%d_%d" % (pb, sl) for sl in range(128)])

            gcount = [0]
            dcount = [0]
            slot_gi = {}

            def issue_gather(b, sl):
                gi = gcount[0] % NG; gcount[0] += 1
                slot_gi[(b, sl)] = gi
                pb = b % 2
                S.add("pool", (lambda e, gi=gi, sl=sl, pb=pb: e.indirect_dma_start(out=ug[gi][:], out_offset=None, in_=uvb,
                               in_offset=bass.IndirectOffsetOnAxis(ap=eid[pb][:, sl:sl + 1], axis=0))),
                      r=["eid%d" % pb] + UVB_ALL, w=["ug%d" % gi], dma="ug%d" % gi)

            def slot_dot(b, sl):
                pb = b % 2; gi = slot_gi[(b, sl)]
                ji = sl % 3
                if sl % 8 == 7:
                    stt("dve", junkb[ji][:], ug[gi][:, 0:D], 1.0, xn2b[pb][:], ALU.mult, ALU.mult,
                        r=["ug%d" % gi, "xn2b%d" % pb, "a%d_%d" % (pb, sl)], w=["a%d_%d" % (pb, sl), "junkb%d" % ji], accum=a_sb[pb][:, sl:sl + 1])
                else:
                    tt("dve", junkb[ji][:], ug[gi][:, 0:D], xn2b[pb][:], ALU.mult, r=["ug%d" % gi, "xn2b%d" % pb], w=["junkb%d" % ji])
                    act(junkb[ji][:], junkb[ji][:], AF.Copy, r=["junkb%d" % ji, "a%d_%d" % (pb, sl)], w=["junkb%d" % ji, "a%d_%d" % (pb, sl)], accum=a_sb[pb][:, sl:sl + 1])
                act(ga[:, sl:sl + 1], a_sb[pb][:, sl:sl + 1], AF.Gelu, r=["a%d_%d" % (pb, sl)], w=["ga_%d" % sl])

            def slot_acc(b, sl):
                pb = b % 2; gi = slot_gi[(b, sl)]
                di = dcount[0] % 4; dcount[0] += 1
                ts("dve", dg[di][:], id_bf[:], ga[:, sl:sl + 1], gate[pb][:, sl:sl + 1], ALU.mult, ALU.mult,
                   r=["ga_%d" % sl, "gate%d" % pb], w=["dg%d" % di])
                mm(P[5][:], dg[di][:], ug[gi][:, D:D + 512], sl == 0, sl == 127, r=["dg%d" % di, "ug%d" % gi], w=[PN[5]])
                mm(P[6][:], dg[di][:], ug[gi][:, D + 512:2 * D], sl == 0, sl == 127, r=["dg%d" % di, "ug%d" % gi], w=[PN[6]])

            def slot_loop(b, fe_ops):
                per = (len(fe_ops) + 111) // 112
                for sl in range(min(LA, 128)):
                    issue_gather(b, sl)
                for sl in range(128):
                    if sl + LA < 128:
                        issue_gather(b, sl + LA)
                    slot_dot(b, sl)
                    if sl >= 2:
                        slot_acc(b, sl - 2)
                    S.flush(fe_ops, per)
                slot_acc(b, 126)
                slot_acc(b, 127)
                S.flush(fe_ops, len(fe_ops))
                oi = b % 2
                tt("dve", ob[oi][:, 0:512], P[5][:], hbuf[:, b, 0:512], ALU.add, r=[PN[5]], w=["ob%d" % oi])
                tt("dve", ob[oi][:, 512:1024], P[6][:], hbuf[:, b, 512:1024], ALU.add, r=[PN[6]], w=["ob%d" % oi])
                dma("sp", out[b * 128:(b + 1) * 128, :], ob[oi][:], r=["ob%d" % oi], key="obo%d" % oi)

            front_end(0)
            for b in range(NB):
                fe_ops = []
                if b + 1 < NB:
                    S.defer_begin()
                    front_end(b + 1)
                    fe_ops = S.defer_end()
                slot_loop(b, fe_ops)
            S.emit()
    return nc


def _t5_bucket(dist):
    dist = np.asarray(dist)
    d_f = np.maximum(dist, 1).astype(np.float32)
    large = 16 + (np.log(d_f / np.float32(16.0)) / np.float32(np.log(128.0 / 16.0)) * np.float32(16.0)).astype(np.int32)
    large = np.minimum(large, 31)
    return np.where(dist < 16, dist, large)


def prep_shared(inp):
    f = np.float32
    sh = {}
    sh["g1col"] = np.ascontiguousarray(inp["ln1_g"][0].reshape(8, 128).T)
    sh["w_in"] = np.ascontiguousarray(inp["w_in"][0])
    sh["bgcol"] = np.ascontiguousarray(inp["b_gate"][0].reshape(16, 128).T)
    lr = inp["s5_lambda_re"][0]; li = inp["s5_lambda_im"][0]; ls = inp["s5_log_step"][0]
    sh["lr_row"] = np.ascontiguousarray(np.broadcast_to(lr.reshape(1, 2048), (128, 2048)))
    sh["li_row"] = np.ascontiguousarray(np.broadcast_to(li.reshape(1, 2048), (128, 2048)))
    sh["ls_row"] = np.ascontiguousarray(np.broadcast_to(np.repeat(ls, 64).reshape(1, 2048), (128, 2048)))
    sh["lr_col"] = np.ascontiguousarray(lr.reshape(16, 2, 64).transpose(1, 2, 0).reshape(128, 16))
    sh["li_col"] = np.ascontiguousarray(li.reshape(16, 2, 64).transpose(1, 2, 0).reshape(128, 16))
    sh["ls_col"] = np.ascontiguousarray(np.broadcast_to(ls.reshape(16, 2, 1), (16, 2, 64)).transpose(1, 2, 0).reshape(128, 16))
    for nm, src in (("bT_re", inp["s5_b_re"][0]), ("bT_im", inp["s5_b_im"][0])):
        a = np.zeros((8, 16, 4, 8, 64), f)
        for g in range(32):
            a[g % 8, :, g // 8, g % 8, :] = src[g].T
        sh[nm] = a.reshape(128, 2048)
    for nm, src in (("cpad_re", inp["s5_c_re"][0]), ("cpad_im", inp["s5_c_im"][0])):
        a = np.zeros((2, 64, 16, 8, 16), f)
        for g in range(32):
            a[g % 2, :, g // 2, g % 8, :] = src[g].T
        sh[nm] = a.reshape(128, 2048)
    a = np.zeros((8, 16, 4, 8, 16), f)
    dsk = inp["s5_d"][0]
    for g in range(32):
        a[g % 8, np.arange(16), g // 8, g % 8, np.arange(16)] = dsk[g]
    sh["ddiag"] = a.reshape(128, 512)
    sh["w_glu"] = np.ascontiguousarray(inp["s5_w_glu"][0]); sh["w_s5b"] = np.ascontiguousarray(inp["w_s5_branch"][0])
    sh["qgcol"] = np.ascontiguousarray(np.tile(inp["q_norm_g"][0], 2).reshape(128, 1))
    sh["kgcol"] = np.ascontiguousarray(np.tile(inp["k_norm_g"][0], 2).reshape(128, 1))
    sh["sinks"] = np.ascontiguousarray(np.broadcast_to(inp["attn_sinks"][0].reshape(1, 16), (128, 16)))
    s_i = np.arange(128)[:, None]; q_i = np.arange(128)[None, :]
    dist = np.stack([128 + q_i - s_i, q_i - s_i], 0)
    tab = inp["rel_bias_table"]
    bias = tab[_t5_bucket(np.maximum(dist, 0))]
    bias = bias.reshape(2, 128, 128, 4, 4).transpose(1, 3, 0, 4, 2)
    sh["bias_t"] = np.ascontiguousarray(bias.reshape(128, 4, 1024)).astype(f)
    valid = (dist >= 0) & (dist < 128)
    m = np.where(valid, 0.0, NEG).astype(f)
    sh["mask_t"] = np.ascontiguousarray(np.broadcast_to(m.transpose(1, 0, 2)[:, :, None, :], (128, 2, 4, 128)).reshape(128, 1024))
    sh["w_attn"] = np.ascontiguousarray(inp["w_attn_branch"][0]); sh["w_out"] = np.ascontiguousarray(inp["w_out"][0])
    sh["g2row"] = np.ascontiguousarray(np.broadcast_to(inp["ln2_g"][0].reshape(1, D), (128, D)))
    sh["w_pq"] = np.ascontiguousarray(inp["peer_w_query"][0])
    sk = inp["peer_sub_keys"][0]
    a = np.zeros((2, 64, 8, 2, 128), f)
    for k in range(2):
        a[k, :, :, k, :] = sk[:, k].transpose(2, 0, 1)
    sh["skbd"] = a.reshape(128, 2048)
    sh["peer_u"] = np.ascontiguousarray(inp["peer_u"][0]); sh["peer_v"] = np.ascontiguousarray(inp["peer_v"][0])
    sh["ident"] = np.eye(128, dtype=f)
    sh["tri"] = (np.arange(128)[:, None] <= np.arange(128)[None, :]).astype(f)
    sh["blk1"] = ((np.arange(128)[:, None] // 64) == (np.arange(128)[None, :] // 64)).astype(f)
    sh["iota_col"] = np.arange(128, dtype=f).reshape(128, 1)
    sh["iota_row"] = np.ascontiguousarray(np.broadcast_to(np.arange(128, dtype=f).reshape(1, 128), (128, 128)))
    return sh


_NC_CACHE = {}


def kernel(**inputs):
    inp = {k: np.asarray(v) for k, v in inputs.items()}
    sh = prep_shared(inp)
    n = 8
    if "nc" not in _NC_CACHE:
        _NC_CACHE["nc"] = build_nc()
    nc = _NC_CACHE["nc"]
    in_maps = []
    for c in range(n):
        m = dict(sh)
        m["x"] = np.ascontiguousarray(inp["x"][c])
        in_maps.append(m)
    res = run_bass_kernel_spmd(nc, in_maps, core_ids=list(range(n)))
    return np.stack([np.asarray(r["out"]).reshape(T, D) for r in res.results], axis=0).astype(np.float32)
```

```python
import contextlib
import numpy as np
import concourse.bass as bass
import concourse.mybir as mybir
from concourse.bass_utils import run_bass_kernel_spmd

F32 = mybir.dt.float32
BF16 = mybir.dt.bfloat16
I32 = mybir.dt.int32
U32 = mybir.dt.uint32
ALU = mybir.AluOpType
AF = mybir.ActivationFunctionType
AX = mybir.AxisListType


class Sched:
    ENG = ("pe", "act", "dve", "pool", "sp")

    def __init__(self, nc, stack):
        self.nc = nc
        self.stack = stack
        self.obj = {"pe": nc.tensor, "act": nc.scalar, "dve": nc.vector,
                    "pool": nc.gpsimd, "sp": nc.sync}
        self.sems = {}
        self.count = {}
        self.lastw = {}
        self.readers = {}
        self.waited = {e: {} for e in self.ENG}
        self.hist = {e: {} for e in self.ENG}
        self.seq = {e: 0 for e in self.ENG}
        self.nwaits = 0
        self.streams = {e: [] for e in self.ENG}
        self.nops = 0
        self.nobarrier = set()
        self.deferred = None
        self.interleave = None

    def defer_begin(self):
        self.deferred = []

    def defer_end(self):
        d, self.deferred = self.deferred, None
        return d

    def flush(self, lst, n):
        for _ in range(min(n, len(lst))):
            e, fn, r, w, dma = lst.pop(0)
            self.add(e, fn, r, w, dma)
        while lst and getattr(lst[0][1], "_sticky", False):
            e, fn, r, w, dma = lst.pop(0)
            self.add(e, fn, r, w, dma)

    def _sem(self, key):
        if key not in self.sems:
            self.sems[key] = self.stack.enter_context(self.nc.semaphore(key))
            self.count[key] = 0
        return self.sems[key]

    def _set(self, e, k, v):
        if self.waited[e].get(k, 0) < v:
            self.waited[e][k] = v
            self.hist[e].setdefault(k, []).append((self.seq[e], v))

    def _learn(self, e, pe_, pseq):
        for k2, lst in self.hist[pe_].items():
            lo, hi = 0, len(lst)
            while lo < hi:
                mid = (lo + hi) // 2
                if lst[mid][0] <= pseq:
                    lo = mid + 1
                else:
                    hi = mid
            if lo > 0:
                self._set(e, k2, lst[lo - 1][1])

    def add(self, e, fn, r=(), w=(), dma=None):
        if self.deferred is not None:
            self.deferred.append((e, fn, tuple(r), tuple(w), dma))
            return None
        self.seq[e] += 1
        deps = []
        for x in r:
            if x in self.lastw:
                deps.append((self.lastw[x], "raw"))
        for x in w:
            if x in self.lastw:
                deps.append((self.lastw[x], "waw"))
            for rd in self.readers.get(x, ()):
                deps.append((rd, "war"))
        need = {}
        for (pe_, key, val, isdma, pseq), kind in deps:
            if pe_ == e and not isdma:
                if e == "pe":
                    continue
            if self.waited[e].get(key, 0) >= val:
                continue
            if key not in need or need[key][0] < val:
                need[key] = (val, pe_, pseq, isdma)
        waits = {}
        for key in sorted(need, key=lambda k: (need[k][3], k)):
            val, pe_, pseq, isdma = need[key]
            if self.waited[e].get(key, 0) >= val:
                continue
            waits[key] = val
            self._set(e, key, val)
            self._learn(e, pe_, pseq)
        if dma is not None:
            key, inc = dma, 16
        else:
            key, inc = "E_" + e, 1
        self._sem(key)
        self.count[key] += inc
        tok = (e, key, self.count[key], dma is not None, self.seq[e])
        for x in w:
            self.lastw[x] = tok
            self.readers[x] = []
        for x in r:
            self.readers.setdefault(x, []).append(tok)
        self.streams[e].append((list(waits.items()), fn, key, inc))
        self.nops += 1
        self.nwaits += len(waits)
        if self.interleave is not None:
            lst, n = self.interleave
            self.interleave = None
            self.flush(lst, n)
            self.interleave = (lst, n)
        return tok

    def barrier(self):
        for e in self.ENG:
            waits = []
            self.seq[e] += 1
            for k, v in self.count.items():
                if k in self.nobarrier:
                    continue
                if v > 0 and self.waited[e].get(k, 0) < v:
                    waits.append((k, v))
                    self._set(e, k, v)
            if waits:
                self.streams[e].append((sorted(waits), None, None, 0))

    def emit(self):
        self.barrier()
        nc = self.nc
        names = {"pe": "tensor", "act": "scalar", "dve": "vector", "pool": "gpsimd", "sp": "sync"}
        with nc.Block() as block:
            for e in self.ENG:
                stream = self.streams[e]

                def body(eng, stream=stream):
                    for waits, fn, key, inc in stream:
                        for k, v in waits:
                            eng.wait_ge(self.sems[k], v)
                        if fn is not None:
                            fn(eng).then_inc(self.sems[key], inc)

                getattr(block, names[e])(body)
        self.streams = {e: [] for e in self.ENG}


D = 1024
T = 2048
NB = T // 128
NCH = T // 512
NEXP = 16384
EPS = 1e-6
TWO_PI = float(2 * np.pi)
NEG = -30000.0


def build_nc(dbg=False, stop_after=None):
    nc = bass.Bass("TRN2", target_bir_lowering=False)

    def din(name, shape, dt=F32):
        return nc.dram_tensor(name, list(shape), dt, kind="ExternalInput").ap()

    def dout(name, shape, dt=F32):
        return nc.dram_tensor(name, list(shape), dt, kind="ExternalOutput").ap()

    x = din("x", [T, D])
    g1col = din("g1col", [128, 8])
    w_in = din("w_in", [D, 4096])
    bgcol = din("bgcol", [128, 16])
    lr_row = din("lr_row", [128, 2048]); li_row = din("li_row", [128, 2048]); ls_row = din("ls_row", [128, 2048])
    lr_col = din("lr_col", [128, 16]); li_col = din("li_col", [128, 16]); ls_col = din("ls_col", [128, 16])
    bT_re = din("bT_re", [128, 2048]); bT_im = din("bT_im", [128, 2048])
    cpad_re = din("cpad_re", [128, 16 * 128]); cpad_im = din("cpad_im", [128, 16 * 128])
    ddiag = din("ddiag", [128, 4 * 128])
    w_glu = din("w_glu", [512, 1024]); w_s5b = din("w_s5b", [512, 1024])
    qgcol = din("qgcol", [128, 1]); kgcol = din("kgcol", [128, 1])
    sinks = din("sinks", [128, 16])
    bias_t = din("bias_t", [128, 4, 2 * 4 * 128])
    mask_t = din("mask_t", [128, 2 * 4 * 128])
    w_attn = din("w_attn", [D, D]); w_out = din("w_out", [D, D])
    g2row = din("g2row", [128, D])
    w_pq = din("w_pq", [D, D])
    skbd = din("skbd", [128, 8 * 256])
    peer_u = din("peer_u", [NEXP, D]); peer_v = din("peer_v", [NEXP, D])
    ident = din("ident", [128, 128]); tri = din("tri", [128, 128]); blk1 = din("blk1", [128, 128])
    iota_col = din("iota_col", [128, 1]); iota_row = din("iota_row", [128, 128])
    out = dout("out", [T, D])
    dbgo = {}
    if dbg:
        dbgo["xnT"] = dout("d_xnT", [128, 8 * T])
        dbgo["yg"] = dout("d_yg", [128, 4 * T])
        dbgo["mix"] = dout("d_mix", [128, 8 * T])
        dbgo["ao"] = dout("d_ao", [128, 8 * T])
        dbgo["h"] = dout("d_h", [128, NB * D])

    with contextlib.ExitStack() as st:
        S = Sched(nc, st)

        def sbuf(stack, name, shape, dt=F32):
            return stack.enter_context(nc.sbuf_tensor(name, list(shape), dt))

        def dma(q, o, i, r=(), w=(), key=None):
            S.add(q, lambda e: e.dma_start(out=o, in_=i), r=r, w=w, dma=key)

        def mm(o, lhsT, rhs, start, stop, r=(), w=()):
            S.add("pe", lambda e: e.matmul(o, lhsT=lhsT, rhs=rhs, start=start, stop=stop), r=r, w=w)

        def tr(o, i, r=(), w=()):
            S.add("pe", lambda e: e.transpose(out=o, in_=i, identity=id_bf[:]), r=r, w=w)

        def act(o, i, func, r=(), w=(), bias=None, scale=None, accum=None, sticky=False):
            kw = {}
            if bias is not None:
                kw["bias"] = bias
            if scale is not None:
                kw["scale"] = scale
            if accum is not None:
                kw["accum_out"] = accum
            fn = lambda e: e.activation(out=o, in_=i, func=func, **kw)
            if sticky:
                fn._sticky = True
            S.add("act", fn, r=r, w=w)

        def tt(q, o, a, b, op, r=(), w=()):
            S.add(q, lambda e: e.tensor_tensor(out=o, in0=a, in1=b, op=op), r=r, w=w)

        def ts(q, o, a, s1, s2, op0, op1=None, r=(), w=(), accum=None):
            kw = {}
            if op1 is not None:
                kw["op1"] = op1
            if accum is not None:
                kw["accum_out"] = accum
            S.add(q, lambda e: e.tensor_scalar(out=o, in0=a, scalar1=s1, scalar2=s2, op0=op0, **kw), r=r, w=w)

        def stt(q, o, a, sc, b, op0, op1, r=(), w=(), accum=None):
            kw = {}
            if accum is not None:
                kw["accum_out"] = accum
            S.add(q, lambda e: e.scalar_tensor_tensor(out=o, in0=a, scalar=sc, in1=b, op0=op0, op1=op1, **kw), r=r, w=w)

        def cp(q, o, i, r=(), w=()):
            if q == "act":
                S.add(q, lambda e: e.copy(out=o, in_=i), r=r, w=w)
            else:
                S.add(q, lambda e: e.tensor_copy(out=o, in_=i), r=r, w=w)

        def wload(dst, src3, nk, w, key):
            tok = None
            for k in range(nk):
                o_, i_ = dst[:, k, :], src3[:, k, :]
                tok = S.add("pool", (lambda e, o_=o_, i_=i_: e.dma_start(out=o_, in_=i_)), w=["%s.%d" % (w[0], k)], dma=key)
            S.lastw[w[0]] = tok
            S.readers[w[0]] = []

        def recip(o, i, r=(), w=()):
            S.add("dve", lambda e: e.reciprocal(out=o, in_=i), r=r, w=w)

        def memset(q, o, val, w=()):
            S.add(q, lambda e: e.memset(o, val), w=w)

        def rsqrt_chain(o, i, scale, r, w, tag):
            ts("dve", o, i, scale, EPS, ALU.mult, ALU.add, r=r, w=w)
            act(o, o, AF.Sqrt, r=w, w=w)
            recip(o, o, r=w, w=w)

        def sincos(sin_o, cos_o, ang, tmpf, tmpi, rs, tag):
            ts("dve", tmpf, ang, 1.0 / TWO_PI, None, ALU.mult, r=rs, w=[tag + "tf"])
            cp("dve", tmpi, tmpf, r=[tag + "tf"], w=[tag + "ti"])
            cp("dve", tmpf, tmpi, r=[tag + "ti"], w=[tag + "tf"])
            stt("dve", tmpf, tmpf, -TWO_PI, ang, ALU.mult, ALU.add, r=[tag + "tf"] + list(rs), w=[tag + "tf"])
            ts("dve", tmpf, tmpf, float(np.pi), float(-np.pi), ALU.min, ALU.max, r=[tag + "tf"], w=[tag + "tf"])
            act(sin_o, tmpf, AF.Sin, r=[tag + "tf"], w=[tag + "os"])
            stt("dve", tmpf, tmpf, -1.0, tmpf, ALU.mult, ALU.max, r=[tag + "tf", tag + "os"], w=[tag + "tf"])
            ts("dve", tmpf, tmpf, -1.0, float(np.pi / 2), ALU.mult, ALU.add, r=[tag + "tf"], w=[tag + "tf"])
            act(cos_o, tmpf, AF.Sin, r=[tag + "tf"], w=[tag + "oc"])

        P = [st.enter_context(nc.psum_tensor("P%d" % i, [128, 512], F32)) for i in range(7)]
        PT = st.enter_context(nc.psum_tensor("PTb", [128, 1024], BF16))
        PN = ["P%d" % i for i in range(7)]
        id_bf = sbuf(st, "id_bf", [128, 128], BF16)
        tri_bf = sbuf(st, "tri_bf", [128, 128], BF16)
        blk1_bf = sbuf(st, "blk1_bf", [128, 128], BF16)
        ones_bf = sbuf(st, "ones_bf", [128, 128], BF16)
        iota_c = sbuf(st, "iota_c", [128, 1])
        iota_r = sbuf(st, "iota_r", [128, 128])
        g1c = sbuf(st, "g1c", [128, 8]); bgc = sbuf(st, "bgc", [128, 16])
        qgc = sbuf(st, "qgc", [128, 1]); kgc = sbuf(st, "kgc", [128, 1])
        g2r = sbuf(st, "g2r", [128, D])
        hbuf = sbuf(st, "hbuf", [128, NB, D])

        dma("pool", id_bf[:], ident, w=["c"], key="setup_pool")
        dma("pool", tri_bf[:], tri, w=["c"], key="setup_pool")
        dma("pool", blk1_bf[:], blk1, w=["c"], key="setup_pool")
        dma("sp", iota_c[:], iota_col, w=["c"], key="setup_sp")
        dma("sp", iota_r[:], iota_row, w=["c"], key="setup_sp")
        dma("sp", g1c[:], g1col, w=["c"], key="setup_sp")
        dma("sp", bgc[:], bgcol, w=["c"], key="setup_sp")
        dma("sp", qgc[:], qgcol, w=["c"], key="setup_sp")
        dma("sp", kgc[:], kgcol, w=["c"], key="setup_sp")
        dma("sp", g2r[:], g2row, w=["c"], key="setup_sp")
        memset("dve", ones_bf[:], 1.0, w=["ones"])
        S.barrier()

        w_in_v = w_in.rearrange("(k p) n -> p k n", p=128)

        uvb = nc.dram_tensor("uvb", [NEXP, 2 * D], BF16, kind="Internal").ap()
        CV = 512
        S.nobarrier.add("cvt")
        cvt_list = []
        for r0 in range(0, NEXP, CV):
            cvt_list.append((uvb[r0:r0 + CV, 0:D], peer_u[r0:r0 + CV, :], "uvbu%d" % r0))
            cvt_list.append((uvb[r0:r0 + CV, D:2 * D], peer_v[r0:r0 + CV, :], "uvbv%d" % r0))

        def issue_cvt(n):
            for _ in range(n):
                if cvt_list:
                    o_, i_, nm_ = cvt_list.pop(0)
                    dma("pool", o_, i_, w=[nm_], key="cvt")
        UVB_ALL = ["uvbu%d" % r0 for r0 in range(0, NEXP, CV)] + ["uvbv%d" % r0 for r0 in range(0, NEXP, CV)]

        with contextlib.ExitStack() as s1:
            xnT = sbuf(s1, "xnT", [128, 8, T], BF16)
            mix = sbuf(s1, "mix", [128, 8, T], BF16)

            ss_a = sbuf(s1, "ss_a", [128, NB])

            def emit_1a():
                mflat = mix[:].rearrange("p k t -> p (k t)")
                xb = [mflat[:, i * 2048:(i + 1) * 2048].bitcast(F32) for i in range(2)]
                xnb = [mflat[:, 4096 + i * 1024:4096 + (i + 1) * 1024] for i in range(2)]
                junk = mflat[:, 6144:8192].bitcast(F32)
                for b in range(NB):
                    i = b % 2
                    dma("sp", xb[i], x[b * 128:(b + 1) * 128, :], w=["xb%d" % i], key="xb%d" % i)
                    act(junk, xb[i], AF.Square, r=["xb%d" % i], w=["junk_a", "ss%d" % b], accum=ss_a[:, b:b + 1])
                    rsqrt_chain(ss_a[:, b:b + 1], ss_a[:, b:b + 1], 1.0 / D, r=["ss%d" % b], w=["ss%d" % b], tag="a")
                    ts("dve", xnb[i], xb[i], ss_a[:, b:b + 1], None, ALU.mult, r=["xb%d" % i, "ss%d" % b], w=["xnb%d" % i])
                    for k in range(8):
                        tr(PT[:, k * 128:(k + 1) * 128], xnb[i][:, k * 128:(k + 1) * 128], r=["xnb%d" % i], w=["PT"])
                    tt("dve", xnT[:, :, b * 128:(b + 1) * 128], PT[:].rearrange("p (k t) -> p k t", k=8),
                       g1c[:].unsqueeze(2).to_broadcast([128, 8, 128]), ALU.mult, r=["PT"], w=["xnT%d" % b])

            S.defer_begin()
            emit_1a()
            ops_1a = S.defer_end()
            XN_ALL = ["xnT%d" % b for b in range(NB)]

            def xn_chunk(c):
                return ["xnT%d" % b for b in range(4 * c, 4 * c + 4)]


            def harena(b0, nb):
                return hbuf[:, b0:b0 + nb, :].rearrange("p a b -> p (a b)")

            with contextlib.ExitStack() as sb_:
                uT = harena(8, 4).bitcast(BF16).rearrange("p (j t) -> p j t", j=4)
                yg = harena(12, 4).bitcast(BF16).rearrange("p (j t) -> p j t", j=4)
                Rre = harena(0, 2); Rim = harena(2, 2)
                Pre = harena(4, 2).rearrange("p (a t) -> p a t", a=16); Pim = harena(6, 2).rearrange("p (a t) -> p a t", a=16)
                with contextlib.ExitStack() as sc:
                    w_u = sbuf(sc, "w_u", [128, 8, 512], BF16)
                    Bbd = sbuf(sc, "Bbd", [128, 4, 2, 512], BF16)
                    Cre = sbuf(sc, "Cre", [128, 16, 128], BF16); Cim = sbuf(sc, "Cim", [128, 16, 128], BF16)
                    Dd = sbuf(sc, "Dd", [128, 4, 128], BF16)
                    a128 = sbuf(sc, "a128", [128, 2, 16])
                    car = sbuf(sc, "car", [128, 2, 16])
                    wload(w_u, w_in_v[:, :, 0:512], 8, ["w_u"], "w_u")
                    dma("pool", Cre[:].rearrange("p a b -> p (a b)"), cpad_re, w=["Cre"], key="Cre")
                    dma("pool", Cim[:].rearrange("p a b -> p (a b)"), cpad_im, w=["Cim"], key="Cim")
                    dma("pool", Dd[:].rearrange("p a b -> p (a b)"), ddiag, w=["Dd"], key="Dd")
                    ts("pool", Cim[:], Cim[:], -1.0, None, ALU.mult, r=["Cim"], w=["Cim"])
                    memset("pool", car[:], 0.0, w=["car0", "car1", "car2", "car3"])
                    with contextlib.ExitStack() as sx:
                        X1, X2, X3, X4 = [sbuf(sx, "X%d" % i, [128, 2048])[:] for i in range(4)]
                        W1, W2, W3, W4 = [harena(8 + 2 * i, 2) for i in range(4)]
                        A_, B_, C_, D_ = Rre, Rim, harena(4, 2), harena(6, 2)
                        TI = A_.bitcast(I32)
                        colp = sbuf(sx, "colp", [128, 6, 16])
                        nsc = sbuf(sx, "nsc", [128, 1])
                        smt = sbuf(sx, "smt", [128, 4, 16]); smi = sbuf(sx, "smi", [128, 16], I32)
                        Z = ["Z"]
                        S.interleave = (ops_1a, 2)
                        dma("sp", W1, ls_row, w=Z, key="sx0"); dma("sp", W2, lr_row, w=Z, key="sx1"); dma("sp", W3, li_row, w=Z, key="sx2")
                        dma("sp", colp[:, 0, :], ls_col, w=Z, key="sx3"); dma("sp", colp[:, 1, :], lr_col, w=Z, key="sx4"); dma("sp", colp[:, 2, :], li_col, w=Z, key="sx5")
                        act(W1, W1, AF.Exp, r=Z, w=Z)
                        tt("dve", W4, W2, W1, ALU.mult, r=Z, w=Z)
                        tt("dve", X1, W3, W1, ALU.mult, r=Z, w=Z)
                        sincos(X2, X3, X1, X4, TI, Z, "Z")
                        act(W1, W4, AF.Exp, r=Z, w=Z)
                        tt("dve", X3, X3, W1, ALU.mult, r=Z, w=Z)
                        tt("dve", X2, X2, W1, ALU.mult, r=Z, w=Z)
                        ts("dve", X3, X3, -1.0, None, ALU.add, r=Z, w=Z)
                        tt("dve", B_, W2, W2, ALU.mult, r=Z, w=Z)
                        tt("dve", C_, W3, W3, ALU.mult, r=Z, w=Z)
                        tt("dve", B_, B_, C_, ALU.add, r=Z, w=Z)
                        recip(B_, B_, r=Z, w=Z)
                        tt("dve", C_, X3, W2, ALU.mult, r=Z, w=Z)
                        tt("dve", D_, X2, W3, ALU.mult, r=Z, w=Z)
                        tt("dve", C_, C_, D_, ALU.add, r=Z, w=Z)
                        tt("dve", C_, C_, B_, ALU.mult, r=Z, w=Z)
                        tt("dve", D_, X2, W2, ALU.mult, r=Z, w=Z)
                        tt("dve", W1, X3, W3, ALU.mult, r=Z, w=Z)
                        tt("dve", D_, D_, W1, ALU.subtract, r=Z, w=Z)
                        tt("dve", D_, D_, B_, ALU.mult, r=Z, w=Z)
                        dma("sp", X2, bT_re, r=Z, w=Z, key="sx6"); dma("sp", X3, bT_im, r=Z, w=Z, key="sx7")
                        v4 = lambda ap: ap.rearrange("p (j n) -> p j n", j=4)
                        tt("dve", W1, X2, C_, ALU.mult, r=Z, w=Z)
                        tt("dve", B_, X3, D_, ALU.mult, r=Z, w=Z)
                        tt("dve", Bbd[:, :, 0, :], v4(W1), v4(B_), ALU.subtract, r=Z, w=Z + ["Bbd"])
                        tt("dve", W1, X2, D_, ALU.mult, r=Z, w=Z)
                        tt("dve", B_, X3, C_, ALU.mult, r=Z, w=Z)
                        tt("dve", Bbd[:, :, 1, :], v4(W1), v4(B_), ALU.add, r=Z, w=Z + ["Bbd"])
                        ts("dve", nsc[:], iota_c[:], -1.0, None, ALU.mult, r=Z, w=Z)
                        ts("dve", B_, X1, iota_c[:, 0:1], None, ALU.mult, r=Z, w=Z)
                        sincos(X2, X3, B_, X4, TI, Z, "Z")
                        act(W1, W4, AF.Exp, r=Z, w=Z, scale=nsc[:, 0:1])
                        tt("dve", A_, X3, W1, ALU.mult, r=Z, w=Z)
                        tt("dve", B_, X2, W1, ALU.mult, r=Z, w=Z)
                        ts("dve", B_, B_, -1.0, None, ALU.mult, r=Z, w=Z)
                        act(colp[:, 0, :], colp[:, 0, :], AF.Exp, r=Z, w=Z)
                        tt("dve", colp[:, 3, :], colp[:, 1, :], colp[:, 0, :], ALU.mult, r=Z, w=Z)
                        tt("dve", colp[:, 4, :], colp[:, 2, :], colp[:, 0, :], ALU.mult, r=Z, w=Z)
                        ts("dve", smt[:, 0, :], colp[:, 4, :], 128.0, None, ALU.mult, r=Z, w=Z)
                        sincos(smt[:, 1, :], smt[:, 2, :], smt[:, 0, :], smt[:, 3, :], smi[:], Z, "Z")
                        act(smt[:, 0, :], colp[:, 3, :], AF.Exp, r=Z, w=Z, scale=128.0)
                        tt("dve", a128[:, 0, :], smt[:, 2, :], smt[:, 0, :], ALU.mult, r=Z, w=Z)
                        tt("dve", a128[:, 1, :], smt[:, 1, :], smt[:, 0, :], ALU.mult, r=Z, w=Z)
                        x3 = lambda ap: ap.rearrange("p (a t) -> p a t", a=16)
                        ir_b = iota_r[:].unsqueeze(1).to_broadcast([128, 16, 128])
                        tt("dve", x3(X2), ir_b, colp[:, 4, :].unsqueeze(2).to_broadcast([128, 16, 128]), ALU.mult, r=Z, w=Z)
                        sincos(X3, W2, X2, X4, W3.bitcast(I32), Z, "Z")
                        tt("dve", x3(X1), ir_b, colp[:, 3, :].unsqueeze(2).to_broadcast([128, 16, 128]), ALU.mult, r=Z, w=Z)
                        act(X1, X1, AF.Exp, r=Z, w=Z)
                        tt("dve", C_, W2, X1, ALU.mult, r=Z, w=Z)
                        tt("dve", D_, X3, X1, ALU.mult, r=Z, w=Z)
                        S.interleave = None
                        S.flush(ops_1a, len(ops_1a))
                        if dbg:
                            dma("pool", dbgo["xnT"], xnT[:].rearrange("p k t -> p (k t)"), r=["xnT%d" % b for b in range(NB)], key="dbg")
                        S.barrier()

                    for j in range(4):
                        for c in range(NCH):
                            pb = P[(j * 4 + c) % 2]; pn = PN[(j * 4 + c) % 2]
                            for k in range(8):
                                mm(pb[:], w_u[:, k, j * 128:(j + 1) * 128], xnT[:, k, c * 512:(c + 1) * 512], k == 0, k == 7,
                                   r=["w_u"] + xn_chunk(c), w=[pn])
                            cp("act", uT[:, j, c * 512:(c + 1) * 512], pb[:], r=[pn], w=["uT%d_%d" % (j, c)])

                    with contextlib.ExitStack() as sw:
                        tq = [[sbuf(sw, "tq%d_%d" % (pp, i), [128, 512]) for i in range(4)] for pp in range(2)]
                        Bt = [sbuf(sw, "Bt%d" % pp, [128, 2, 512], BF16) for pp in range(2)]
                        tp = [sbuf(sw, "tp%d" % i, [128, 4, 128]) for i in range(4)]
                        Hh = [sbuf(sw, "Hh%d" % i_, [128, 2, 4, 128], BF16) for i_ in range(2)]
                        gs = sbuf(sw, "gs", [128, 2, 4]); gt = sbuf(sw, "gt", [128, 4, 4])
                        Gc = sbuf(sw, "Gc", [128, 2, 4, 128])
                        its = [(b, j) for b in range(NB) for j in range(4)]

                        def stageA(it):
                            b, j = its[it]; pp = it % 2
                            issue_cvt(1)
                            bre, bim = P[2 * pp], P[2 * pp + 1]; nre, nim = PN[2 * pp], PN[2 * pp + 1]
                            blk = slice(b * 128, (b + 1) * 128)
                            ur = ["uT%d_%d" % (j, b // 4), "Bbd"]
                            mm(bre[:], uT[:, j, blk], Bbd[:, j, 0, :], True, True, r=ur, w=[nre])
                            mm(bim[:], uT[:, j, blk], Bbd[:, j, 1, :], True, True, r=ur, w=[nim])
                            rs_ = slice(j * 512, (j + 1) * 512)
                            t = tq[pp]
                            tt("dve", t[0][:], bre[:], Rre[:, rs_], ALU.mult, r=[nre], w=["tq%d_0" % pp])
                            tt("dve", t[1][:], bim[:], Rim[:, rs_], ALU.mult, r=[nim], w=["tq%d_1" % pp])
                            tt("dve", t[2][:], bre[:], Rim[:, rs_], ALU.mult, r=[nre], w=["tq%d_2" % pp])
                            tt("dve", t[3][:], bim[:], Rre[:, rs_], ALU.mult, r=[nim], w=["tq%d_3" % pp])
                            tt("pool", Bt[pp][:, 0, :], t[0][:], t[1][:], ALU.subtract, r=["tq%d_0" % pp, "tq%d_1" % pp], w=["Bt%d_0" % pp])
                            tt("pool", Bt[pp][:, 1, :], t[2][:], t[3][:], ALU.add, r=["tq%d_2" % pp, "tq%d_3" % pp], w=["Bt%d_1" % pp])

                        def stageB(it):
                            b, j = its[it]; pp = it % 2
                            G = [P[4], P[5]]; GN = [PN[4], PN[5]]
                            for ri in range(2):
                                for q in range(4):
                                    mm(G[ri][:, q * 128:(q + 1) * 128], Bt[pp][:, ri, q * 128:(q + 1) * 128], tri_bf[:], True, True,
                                       r=["Bt%d_%d" % (pp, ri)], w=[GN[ri]])
                            cn = "car%d" % j
                            for ri in range(2):
                                for q in range(4):
                                    Tq = 4 * j + q
                                    act(Gc[:, ri, q, :], G[ri][:, q * 128:(q + 1) * 128], AF.Identity, r=[GN[ri], cn], w=["Gc%d" % ri], bias=car[:, ri, Tq:Tq + 1])
                            f3 = lambda ap: ap.rearrange("p q t -> p (q t)")
                            js = slice(4 * j, 4 * j + 4)
                            tt("dve", f3(tp[0][:]), f3(Gc[:, 0]), f3(Pre[:, js, :]), ALU.mult, r=["Gc0"], w=["tp0"])
                            tt("dve", f3(tp[1][:]), f3(Gc[:, 1]), f3(Pim[:, js, :]), ALU.mult, r=["Gc1"], w=["tp1"])
                            tt("dve", f3(tp[2][:]), f3(Gc[:, 0]), f3(Pim[:, js, :]), ALU.mult, r=["Gc0"], w=["tp2"])
                            tt("dve", f3(tp[3][:]), f3(Gc[:, 1]), f3(Pre[:, js, :]), ALU.mult, r=["Gc1"], w=["tp3"])
                            for ri in range(2):
                                cp("pool", gs[:, ri, :], Gc[:, ri, :, 127], r=["Gc%d" % ri], w=["gs%d" % ri])
                            tt("pool", gt[:, 0, :], gs[:, 0, :], a128[:, 0, js], ALU.mult, r=["gs0"], w=["gt0"])
                            tt("pool", gt[:, 1, :], gs[:, 1, :], a128[:, 1, js], ALU.mult, r=["gs1"], w=["gt1"])
                            tt("pool", gt[:, 2, :], gs[:, 0, :], a128[:, 1, js], ALU.mult, r=["gs0"], w=["gt2"])
                            tt("pool", gt[:, 3, :], gs[:, 1, :], a128[:, 0, js], ALU.mult, r=["gs1"], w=["gt3"])
                            tt("pool", car[:, 0, js], gt[:, 0, :], gt[:, 1, :], ALU.subtract, r=["gt0", "gt1"], w=[cn])
                            tt("pool", car[:, 1, js], gt[:, 2, :], gt[:, 3, :], ALU.add, r=["gt2", "gt3"], w=[cn])
                            f2 = lambda ap: ap.rearrange("p q t -> p (q t)")
                            tt("dve", f2(Hh[pp][:, 0]), f2(tp[0][:]), f2(tp[1][:]), ALU.subtract, r=["tp0", "tp1"], w=["Hh%d_0" % pp])
                            tt("dve", f2(Hh[pp][:, 1]), f2(tp[2][:]), f2(tp[3][:]), ALU.add, r=["tp2", "tp3"], w=["Hh%d_1" % pp])

                        def stageC(it):
                            b, j = its[it]; pp = it % 2
                            blk = slice(b * 128, (b + 1) * 128)
                            Y = P[6]
                            for q in range(4):
                                Tq = 4 * j + q
                                mm(Y[:, 0:128], Cre[:, Tq, :], Hh[pp][:, 0, q, :], q == 0, False, r=["Cre", "Hh%d_0" % pp], w=[PN[6]])
                                mm(Y[:, 0:128], Cim[:, Tq, :], Hh[pp][:, 1, q, :], False, False, r=["Cim", "Hh%d_1" % pp], w=[PN[6]])
                            mm(Y[:, 0:128], Dd[:, j, :], uT[:, j, blk], False, True, r=["Dd", "uT%d_%d" % (j, b // 4)], w=[PN[6]])
                            act(yg[:, j, blk], Y[:, 0:128], AF.Gelu, r=[PN[6]], w=["yg%d_%d" % (j, b)])

                        stageA(0)
                        for it in range(len(its)):
                            if it + 1 < len(its):
                                stageA(it + 1)
                            stageB(it)
                            if it >= 1:
                                stageC(it - 1)
                        stageC(len(its) - 1)
                        issue_cvt(len(cvt_list))
                        S.barrier()
                    S.emit()
                if stop_after == "scan":
                    dma("pool", dbgo["yg"], yg.rearrange("p j t -> p (j t)"), key="dbg")
                    S.emit()
                    return nc

                with contextlib.ExitStack() as sg:
                    w_gl = sbuf(sg, "w_gl", [128, 4, 1024], BF16)
                    s5o = sbuf(sg, "s5o", [128, 4, T], BF16)
                    sig = [sbuf(sg, "sig%d" % i, [128, 512]) for i in range(2)]
                    w_gs = sbuf(sg, "w_gs", [128, 8, 1024], BF16)
                    w_sb = sbuf(sg, "w_sb", [128, 4, 1024], BF16)
                    wload(w_gl, w_glu.rearrange("(k p) n -> p k n", p=128), 4, ["w_gl"], "w_gl")
                    wload(w_gs, w_in_v[:, :, 2048:3072], 8, ["w_gs"], "w_gs")
                    wload(w_sb, w_s5b.rearrange("(k p) n -> p k n", p=128), 4, ["w_sb"], "w_sb")
                    for n in range(4):
                        for c in range(NCH):
                            i = (n * 4 + c) % 2
                            pa, pb = P[2 * i], P[2 * i + 1]; na, nb_ = PN[2 * i], PN[2 * i + 1]
                            cs_ = slice(c * 512, (c + 1) * 512)
                            for k in range(4):
                                mm(pa[:], w_gl[:, k, n * 128:(n + 1) * 128], yg[:, k, cs_], k == 0, k == 3, r=["w_gl"], w=[na])
                            for k in range(4):
                                mm(pb[:], w_gl[:, k, 512 + n * 128:512 + (n + 1) * 128], yg[:, k, cs_], k == 0, k == 3, r=["w_gl"], w=[nb_])
                            act(sig[i][:], pb[:], AF.Sigmoid, r=[nb_], w=["sig%d" % i])
                            tt("dve", s5o[:, n, cs_], pa[:], sig[i][:], ALU.mult, r=[na, "sig%d" % i], w=["s5o_%d" % c])
                    for n in range(8):
                        for c in range(NCH):
                            i = (n * 4 + c) % 2
                            pg, py = P[2 * i], P[2 * i + 1]; ng, ny = PN[2 * i], PN[2 * i + 1]
                            cs_ = slice(c * 512, (c + 1) * 512)
                            for k in range(8):
                                mm(pg[:], w_gs[:, k, n * 128:(n + 1) * 128], xnT[:, k, cs_], k == 0, k == 7, r=["w_gs"], w=[ng])
                            for k in range(4):
                                mm(py[:], w_sb[:, k, n * 128:(n + 1) * 128], s5o[:, k, cs_], k == 0, k == 3, r=["w_sb", "s5o_%d" % c], w=[ny])
                            act(sig[i][:], pg[:], AF.Sigmoid, r=[ng], w=["sig%d" % i], bias=bgc[:, n:n + 1])
                            tt("dve", mix[:, n, cs_], py[:], sig[i][:], ALU.mult, r=[ny, "sig%d" % i], w=["mix%d_%d" % (n, c)])
                    if dbg:
                        dma("pool", dbgo["yg"], s5o[:].rearrange("p j t -> p (j t)"), r=["s5o_%d" % c for c in range(NCH)], key="dbg")
                        if stop_after == "1b":
                            dma("pool", dbgo["mix"], mix[:].rearrange("p j t -> p (j t)"), r=["mix%d_%d" % (n, c) for n in range(8) for c in range(NCH)], key="dbg")
                    S.emit()
            if stop_after == "1b":
                return nc

            aoT = harena(0, 8).bitcast(BF16).rearrange("p (j t) -> p j t", j=8)
            with contextlib.ExitStack() as sc_:
                qbd = harena(8, 4).bitcast(BF16).rearrange("p (j b c) -> p j b c", j=2, b=NB)
                kT_a = harena(12, 1).bitcast(BF16)
                v_a = harena(13, 1).bitcast(BF16).rearrange("p (b n) -> p b n", b=NB)
                BMa = [harena(14 + i, 1) for i in range(2)]
                maskt = sbuf(sc_, "maskt", [128, 1024])[:]
                memset("pool", harena(8, 4).bitcast(BF16), 0.0, w=["qbd"])
                wgrp = [sbuf(sc_, "wgrp%d" % i, [128, 8, 512], BF16) for i in range(2)]
                sq = [sbuf(sc_, "sq%d" % i, [128, 512], BF16) for i in range(2)]
                rs = [sbuf(sc_, "rs%d" % i, [128, 512]) for i in range(2)]
                tmpS = [sbuf(sc_, "tmpS%d" % i, [128, 2, 512]) for i in range(2)]
                PTb = [sbuf(sc_, "PTb_s%d" % i, [128, 2, 512], BF16) for i in range(2)]
                rc = [sbuf(sc_, "rc%d" % i, [128, 512]) for i in range(2)]
                PTf = PT[:].bitcast(F32)
                sk_sb = sbuf(sc_, "sk_sb", [128, 16])
                esink = sbuf(sc_, "esink", [128, 16 * 128])
                qgs = sbuf(sc_, "qgs", [128, 1])
                epsc = sbuf(sc_, "epsc", [128, 1])
                memset("pool", epsc[:], EPS, w=["epsc"])
                dma("sp", maskt, mask_t, w=["maskt"], key="maskt")
                dma("sp", sk_sb[:], sinks, w=["sk_sb"], key="sk_sb")
                act(sk_sb[:], sk_sb[:], AF.Exp, r=["sk_sb"], w=["sk_sb"])
                cp("dve", esink[:].rearrange("p (h q) -> p h q", h=16), sk_sb[:].unsqueeze(2).to_broadcast([128, 16, 128]), r=["sk_sb"], w=["esink"])
                ts("dve", qgs[:], qgc[:], 0.125, None, ALU.mult, w=["qgs"])

                def load_grp(a):
                    wg = wgrp[a % 2]; nm = "wgrp%d" % (a % 2)
                    dma("pool", wg[:, :, 0:256], w_in_v[:, :, 512 + a * 256:512 + (a + 1) * 256], w=[nm], key=nm)
                    for dup in range(2):
                        dma("pool", wg[:, :, 256 + dup * 64:320 + dup * 64], w_in_v[:, :, 1536 + a * 64:1536 + (a + 1) * 64], w=[nm], key=nm)
                        dma("pool", wg[:, :, 384 + dup * 64:448 + dup * 64], w_in_v[:, :, 1792 + a * 64:1792 + (a + 1) * 64], w=[nm], key=nm)
                    bn = "BMa%d" % (a % 2)
                    dma("sp", BMa[a % 2], bias_t[:, a, :], w=[bn], key=bn)
                    tt("pool", BMa[a % 2], BMa[a % 2], maskt, ALU.add, r=[bn, "maskt"], w=[bn])

                load_grp(0)
                for a in range(4):
                    if a + 1 < 4:
                        load_grp(a + 1)
                    wg = wgrp[a % 2]; nm = "wgrp%d" % (a % 2); bn = "BMa%d" % (a % 2)
                    for (dst, col0, gcol, tag) in ((0, 0, qgs, "q"), (1, 128, qgs, "q"), (2, 256, kgc, "k")):
                        for c in range(NCH):
                            cs_ = slice(c * 512, (c + 1) * 512)
                            pq = P[c % 2]; nq = PN[c % 2]
                            for k in range(8):
                                mm(pq[:], wg[:, k, col0:col0 + 128], xnT[:, k, cs_], k == 0, k == 7, r=[nm], w=[nq])
                            ci = c % 2
                            act(sq[ci][:], pq[:], AF.Square, r=[nq], w=["sq%d" % ci])
                            mm(P[2 + ci][:], blk1_bf[:], sq[ci][:], True, True, r=["sq%d" % ci], w=[PN[2 + ci]])
                            act(rs[ci][:], P[2 + ci][:], AF.Ln, r=[PN[2 + ci], "epsc"], w=["rs%d" % ci], scale=1.0 / 64, bias=epsc[:, 0:1])
                            act(rs[ci][:], rs[ci][:], AF.Exp, r=["rs%d" % ci], w=["rs%d" % ci], scale=-0.5)
                            if dst < 2:
                                for half in range(2):
                                    hs = slice(half * 64, half * 64 + 64)
                                    v4b = lambda ap: ap.rearrange("p (b q) -> p b q", b=4)
                                    stt("dve", qbd[hs, dst, 4 * c:4 * c + 4, half * 128:(half + 1) * 128], v4b(pq[hs, :]), gcol[hs, 0:1], v4b(rs[ci][hs, :]),
                                        ALU.mult, ALU.mult, r=[nq, "rs%d" % ci, "qgs", "qbd"], w=["qk%d_%d" % (dst, c)])
                            else:
                                stt("dve", kT_a[:, cs_], pq[:], gcol[:, 0:1], rs[ci][:], ALU.mult, ALU.mult, r=[nq, "rs%d" % ci, "qgs"], w=["qk%d_%d" % (dst, c)])
                    for b4 in range(4):
                        pv = P[6]
                        for bb in range(4):
                            blk = b4 * 4 + bb
                            for k in range(8):
                                mm(pv[:, bb * 128:(bb + 1) * 128], xnT[:, k, blk * 128:(blk + 1) * 128], wg[:, k, 384:512], k == 0, k == 7, r=[nm], w=[PN[6]])
                        cp("act", v_a[:, b4 * 4:(b4 + 1) * 4, :], pv[:].rearrange("p (b n) -> p b n", b=4), r=[PN[6]], w=["v_a%d" % b4])
                    SBK = {(0, 0): 0, (0, 1): 1, (1, 0): 4, (1, 1): 5}

                    def att_stage1(n):
                        p = n % 2
                        nbk = slice(n * 128, (n + 1) * 128)
                        pbk = slice((n - 1) * 128, n * 128)
                        qr = ["qk0_%d" % (n // 4), "qk1_%d" % (n // 4)]
                        kbs = [1] if n == 0 else [0, 1]
                        for kb in kbs:
                            bk = SBK[(p, kb)]
                            kblk = nbk if kb == 1 else pbk
                            for j2 in range(2):
                                mm(P[bk][:, j2 * 256:(j2 + 1) * 256], kT_a[:, kblk], qbd[:, j2, n, :], True, True,
                                   r=qr + ["qk2_%d" % ((n - 1 + kb) // 4)], w=[PN[bk]])
                        for kb in kbs:
                            bk = SBK[(p, kb)]
                            tt("dve", tmpS[p][:, kb, :], P[bk][:], BMa[a % 2][:, kb * 512:(kb + 1) * 512], ALU.add, r=[PN[bk], bn], w=["tmpS%d_%d" % (p, kb)])
                        k0 = kbs[0]
                        act(PTb[p][:, k0:2, :], tmpS[p][:, k0:2, :], AF.Exp, r=["tmpS%d_%d" % (p, kb) for kb in kbs], w=["PTb%d_0" % p, "PTb%d_1" % p])

                    def att_stage2(n):
                        p = n % 2
                        nbk = slice(n * 128, (n + 1) * 128)
                        kbs = [1] if n == 0 else [0, 1]
                        Nn, nn_ = (P[6][:], PN[6]) if p == 0 else (PTf, "PT")
                        Dn, dn_ = (P[2][:], PN[2]) if p == 0 else (P[3][:], PN[3])
                        rcp, rn_ = rc[p], "rc%d" % p
                        for idx, kb in enumerate(kbs):
                            blk = n - 1 + kb
                            mm(Nn, v_a[:, blk, :], PTb[p][:, kb, :], idx == 0, idx == len(kbs) - 1, r=["v_a%d" % (blk // 4), "PTb%d_0" % p, "PTb%d_1" % p], w=[nn_])
                            mm(Dn, ones_bf[:], PTb[p][:, kb, :], idx == 0, idx == len(kbs) - 1, r=["PTb%d_0" % p, "PTb%d_1" % p], w=[dn_])
                        tt("dve", rcp[:], Dn, esink[:, a * 512:(a + 1) * 512], ALU.add, r=[dn_, "esink"], w=[rn_])
                        act(rcp[:], rcp[:], AF.Ln, r=[rn_], w=[rn_])
                        act(rcp[:], rcp[:], AF.Exp, r=[rn_], w=[rn_], scale=-1.0)
                        for j2 in range(2):
                            for half in range(2):
                                hs = slice(half * 64, half * 64 + 64)
                                cc = slice((2 * j2 + half) * 128, (2 * j2 + half + 1) * 128)
                                tt("dve", aoT[hs, 2 * a + j2, nbk], Nn[hs, cc], rcp[hs, cc], ALU.mult, r=[nn_, rn_], w=["ao%d" % (2 * a + j2)])

                    att_stage1(0)
                    for n in range(NB):
                        if n + 1 < NB:
                            att_stage1(n + 1)
                        att_stage2(n)
                if dbg:
                    dma("pool", dbgo["ao"], aoT.rearrange("p j t -> p (j t)"), r=["ao%d" % i for i in range(8)], key="dbg")
                S.emit()
            if stop_after == "1c":
                return nc

            with contextlib.ExitStack() as sd:
                w_ga = sbuf(sd, "w_ga", [128, 8, 1024], BF16)
                w_at = sbuf(sd, "w_at", [128, 8, 1024], BF16)
                w_o = sbuf(sd, "w_o", [128, 8, 1024], BF16)
                xb = [sbuf(sd, "xbe%d" % i, [128, D]) for i in range(2)]
                sig = [sbuf(sd, "sigd%d" % i, [128, 512]) for i in range(2)]
                tm = [sbuf(sd, "tmd%d" % i, [128, 512]) for i in range(2)]
                wload(w_ga, w_in_v[:, :, 3072:4096], 8, ["w_ga"], "w_ga")
                wload(w_at, w_attn.rearrange("(k p) n -> p k n", p=128), 8, ["w_at"], "w_at")
                wload(w_o, w_out.rearrange("(k p) n -> p k n", p=128), 8, ["w_o"], "w_o")
                for i in range(2):
                    dma("sp", xb[i][:], x[i * 128:(i + 1) * 128, :], w=["xbe%d" % i], key="xbe%d" % i)
                for n in range(8):
                    for c in range(NCH):
                        i = (n * 4 + c) % 2
                        pg, py = P[2 * i], P[2 * i + 1]; ng, ny = PN[2 * i], PN[2 * i + 1]
                        cs_ = slice(c * 512, (c + 1) * 512)
                        for k in range(8):
                            mm(pg[:], w_ga[:, k, n * 128:(n + 1) * 128], xnT[:, k, cs_], k == 0, k == 7, r=["w_ga"], w=[ng])
                        for k in range(8):
                            mm(py[:], w_at[:, k, n * 128:(n + 1) * 128], aoT[:, k, cs_], k == 0, k == 7, r=["w_at"], w=[ny])
                        act(sig[i][:], pg[:], AF.Sigmoid, r=[ng], w=["sigd%d" % i], bias=bgc[:, 8 + n:9 + n])
                        tt("dve", tm[i][:], py[:], sig[i][:], ALU.mult, r=[ny, "sigd%d" % i], w=["tmd%d" % i])
                        tt("pool", mix[:, n, cs_], tm[i][:], mix[:, n, cs_], ALU.add, r=["tmd%d" % i, "mix%d_%d" % (n, c)], w=["mix%d_%d" % (n, c)])
                if dbg:
                    dma("pool", dbgo["mix"], mix[:].rearrange("p j t -> p (j t)"), r=["mix%d_%d" % (n, c) for n in range(8) for c in range(NCH)], key="dbg")
                S.barrier()
                for blk in range(NB):
                    i = blk % 2
                    bs_ = slice(blk * 128, (blk + 1) * 128)
                    if blk >= 2:
                        dma("sp", xb[i][:], x[bs_, :], w=["xbe%d" % i], key="xbe%d" % i)
                    for half in range(2):
                        ph = P[(2 * blk + half) % 4]; nh = PN[(2 * blk + half) % 4]
                        hc = slice(half * 512, (half + 1) * 512)
                        for k in range(8):
                            mm(ph[:], mix[:, k, bs_], w_o[:, k, hc], k == 0, k == 7, r=["w_o"], w=[nh])
                        tt("dve", hbuf[:, blk, hc], ph[:], xb[i][:, hc], ALU.add, r=[nh, "xbe%d" % i], w=["h%d" % blk])
                if dbg:
                    dma("sp", dbgo["h"], hbuf[:].rearrange("p b d -> p (b d)"), r=["h%d" % b for b in range(NB)], key="dbg")
                S.emit()
        if stop_after == "1d":
            return nc

        NG = 10
        LA = 5
        with contextlib.ExitStack() as s2:
            wq = sbuf(s2, "wq", [128, 8, 1024], BF16)
            sk = sbuf(s2, "sk", [128, 8, 256], BF16)
            xn2b = [sbuf(s2, "xn2b%d" % i, [128, D], BF16) for i in range(2)]
            xn2T = sbuf(s2, "xn2T", [128, 8, 128], BF16)
            qpT = sbuf(s2, "qpT", [128, 8, 128], BF16)
            s_sb = sbuf(s2, "s_sb", [128, 16, 128])
            work = sbuf(s2, "work", [128, 16, 128])
            vals = sbuf(s2, "vals", [128, 16, 16])
            idxu = sbuf(s2, "idxu", [128, 16, 16], U32)
            idxf = sbuf(s2, "idxf", [128, 16, 16])
            cand = sbuf(s2, "cand", [128, 8, 256])
            work2 = work[:].rearrange("p (h a) n -> p h (a n)", a=2)
            best = sbuf(s2, "best", [128, 8, 16])
            posu = sbuf(s2, "posu", [128, 8, 16], U32)
            posf = sbuf(s2, "posf", [128, 8, 16]); rf = sbuf(s2, "rf", [128, 8, 16]); cf = sbuf(s2, "cf", [128, 8, 16])
            ri_ = sbuf(s2, "ri_", [128, 8, 16], I32)
            oh = sbuf(s2, "oh", [128, 8, 16, 16])
            i1 = sbuf(s2, "i1", [128, 8, 16]); i2 = sbuf(s2, "i2", [128, 8, 16])
            eidf = sbuf(s2, "eidf", [128, 128])
            eid = [sbuf(s2, "eid%d" % i, [128, 128], I32) for i in range(2)]
            negm = sbuf(s2, "negm", [128, 8]); zz = sbuf(s2, "zz", [128, 8])
            ee = sbuf(s2, "ee", [128, 8, 16])
            gate = [sbuf(s2, "gate%d" % i, [128, 128]) for i in range(2)]
            a_sb = [sbuf(s2, "a_sb%d" % i, [128, 128]) for i in range(2)]
            ga = sbuf(s2, "ga", [128, 128]); wc = sbuf(s2, "wc", [128, 128])
            ss2 = sbuf(s2, "ss2", [128, NB])
            junk = oh[:].rearrange("p a b c -> p (a b c)")[:, 0:D]
            junkb = [sbuf(s2, "junkb%d" % i, [128, D], BF16) for i in range(3)]
            ug = [sbuf(s2, "ug%d" % i, [128, 2 * D], BF16) for i in range(NG)]
            dg = [sbuf(s2, "dg%d" % i, [128, 128], BF16) for i in range(4)]
            ob = [sbuf(s2, "ob%d" % i, [128, D]) for i in range(2)]
            wload(wq, w_pq.rearrange("(k p) n -> p k n", p=128), 8, ["wq"], "wq")
            dma("pool", sk[:].rearrange("p h n -> p (h n)"), skbd, w=["sk"], key="sk")

            def top16(dst_v, dst_i, lists, wk, nl, tag, lres):
                for l in range(nl):
                    S.add("dve", (lambda e, l=l: e.max(out=dst_v[:, l, 0:8], in_=lists[:, l, :])), r=[lres], w=[tag + "v0_%d" % l])
                for l in range(nl):
                    S.add("dve", (lambda e, l=l: e.max_index(out=dst_i[:, l, 0:8], in_max=dst_v[:, l, 0:8], in_values=lists[:, l, :])),
                          r=[lres, tag + "v0_%d" % l], w=[tag + "i0_%d" % l])
                wn = lambda l: ["wk_%d" % l] if nl == 16 else ["wk_%d" % (2 * l), "wk_%d" % (2 * l + 1)]
                for l in range(nl):
                    S.add("dve", (lambda e, l=l: e.match_replace(out=wk[:, l, :], in_to_replace=dst_v[:, l, 0:8], in_values=lists[:, l, :], imm_value=-1e30)),
                          r=[lres, tag + "v0_%d" % l], w=wn(l))
                for l in range(nl):
                    S.add("dve", (lambda e, l=l: e.max(out=dst_v[:, l, 8:16], in_=wk[:, l, :])), r=wn(l), w=[tag + "v1_%d" % l])
                for l in range(nl):
                    S.add("dve", (lambda e, l=l: e.max_index(out=dst_i[:, l, 8:16], in_max=dst_v[:, l, 8:16], in_values=wk[:, l, :])),
                          r=wn(l) + [tag + "v1_%d" % l], w=[tag + "i1_%d" % l])
                return [tag + "v0_%d" % l for l in range(nl)] + [tag + "v1_%d" % l for l in range(nl)], \
                       [tag + "i0_%d" % l for l in range(nl)] + [tag + "i1_%d" % l for l in range(nl)]

            def gather_by(idx_f, sel_f, dst, tag):
                io = iota_r[:, 0:16].unsqueeze(1).unsqueeze(1).to_broadcast([128, 8, 16, 16])
                tt("dve", oh[:], io, sel_f[:].unsqueeze(3).to_broadcast([128, 8, 16, 16]), ALU.is_equal, r=[tag + "sel"], w=["oh"])
                tt("pool", oh[:], oh[:], idx_f.unsqueeze(2).to_broadcast([128, 8, 16, 16]), ALU.mult, r=["oh", "idxf"], w=["oh"])
                S.add("dve", lambda e: e.tensor_reduce(out=dst[:], in_=oh[:], axis=AX.X, op=ALU.add), r=["oh"], w=[tag + "dst"])

            def front_end(b):
                pb = b % 2
                hb = hbuf[:, b, :]
                act(junk, hb, AF.Square, w=["oh", "ss2_%d" % b], accum=ss2[:, b:b + 1])
                rsqrt_chain(ss2[:, b:b + 1], ss2[:, b:b + 1], 1.0 / D, r=["ss2_%d" % b], w=["ss2_%d" % b], tag="p")
                stt("dve", xn2b[pb][:], hb, ss2[:, b:b + 1], g2r[:], ALU.mult, ALU.mult, r=["ss2_%d" % b], w=["xn2b%d" % pb])
                for k in range(8):
                    tr(PT[:, k * 128:(k + 1) * 128], xn2b[pb][:, k * 128:(k + 1) * 128], r=["xn2b%d" % pb], w=["PT"])
                cp("act", xn2T[:].rearrange("p k t -> p (k t)"), PT[:], r=["PT"], w=["xn2T"])
                for h in range(8):
                    pb_ = P[h // 4]
                    for k in range(8):
                        mm(pb_[:, (h % 4) * 128:(h % 4 + 1) * 128], wq[:, k, h * 128:(h + 1) * 128], xn2T[:, k, :], k == 0, k == 7, r=["wq", "xn2T"], w=[PN[h // 4]])
                for hh in range(2):
                    cp("act", qpT[:, hh * 4:(hh + 1) * 4, :].rearrange("p h t -> p (h t)"), P[hh][:], r=[PN[hh]], w=["qpT%d" % hh])
                for h in range(8):
                    bk = 2 + (h // 2) % 2
                    mm(P[bk][:, (h % 2) * 256:(h % 2 + 1) * 256], qpT[:, h, :], sk[:, h, :], True, True, r=["qpT%d" % (h // 4), "sk"], w=[PN[bk]])
                    if h % 2 == 1:
                        i = h // 2
                        cp("act", s_sb[:, 4 * i:4 * i + 4, :].rearrange("p l n -> p (l n)"), P[bk][:], r=[PN[bk]], w=["sL"])
                vr, ir = top16(vals, idxu, s_sb, work, 16, "s", "sL")
                cp("dve", idxf[:], idxu[:], r=ir, w=["idxf"])
                v3 = vals[:].rearrange("p (h k) r -> p h k r", k=2)
                tt("dve", cand[:].rearrange("p h (r c) -> p h r c", r=16), v3[:, :, 0, :].unsqueeze(3).to_broadcast([128, 8, 16, 16]),
                   v3[:, :, 1, :].unsqueeze(2).to_broadcast([128, 8, 16, 16]), ALU.add, r=vr, w=["cL"])
                vr2, ir2 = top16(best, posu, cand, work2, 8, "c", "cL")
                cp("dve", posf[:], posu[:], r=ir2, w=["posf"])
                ts("dve", rf[:], posf[:], -7.5, 1.0 / 16, ALU.add, ALU.mult, r=["posf"], w=["rf"])
                cp("dve", ri_[:], rf[:], r=["rf"], w=["ri_"])
                cp("dve", rf[:], ri_[:], r=["ri_"], w=["rsel"])
                stt("dve", cf[:], rf[:], -16.0, posf[:], ALU.mult, ALU.add, r=["rsel", "posf"], w=["csel"])
                i3 = idxf[:].rearrange("p (h k) r -> p h k r", k=2)
                gather_by(i3[:, :, 0, :], rf, i1, "r")
                gather_by(i3[:, :, 1, :], cf, i2, "c")
                stt("dve", eidf[:].rearrange("p (h j) -> p h j", h=8), i1[:], 128.0, i2[:], ALU.mult, ALU.add, r=["rdst", "cdst"], w=["eidf"])
                cp("dve", eid[pb][:], eidf[:], r=["eidf"], w=["eid%d" % pb])
                ts("dve", negm[:], best[:, :, 0], -1.0, None, ALU.mult, r=vr2, w=["negm"])
                for h in range(8):
                    act(ee[:, h, :], best[:, h, :], AF.Exp, r=vr2 + ["negm"], w=["ee%d" % h, "zz%d" % h], bias=negm[:, h:h + 1], accum=zz[:, h:h + 1], sticky=(h > 0))
                recip(zz[:], zz[:], r=["zz%d" % h for h in range(8)], w=["rz"])
                tt("dve", gate[pb][:].rearrange("p (h j) -> p h j", h=8), ee[:], zz[:].unsqueeze(2).to_broadcast([128, 8, 16]), ALU.mult,
                   r=["rz"] + ["ee%d" % h for h in range(8)], w=["gate%d" % pb])
                memset("pool", a_sb[pb][:], 0.0, w=["a%d_%d" % (pb, sl) for sl in range(128)])

            gcount = [0]
            dcount = [0]
            slot_gi = {}

            def issue_gather(b, sl):
                gi = gcount[0] % NG; gcount[0] += 1
                slot_gi[(b, sl)] = gi
                pb = b % 2
                S.add("pool", (lambda e, gi=gi, sl=sl, pb=pb: e.indirect_dma_start(out=ug[gi][:], out_offset=None, in_=uvb,
                               in_offset=bass.IndirectOffsetOnAxis(ap=eid[pb][:, sl:sl + 1], axis=0))),
                      r=["eid%d" % pb] + UVB_ALL, w=["ug%d" % gi], dma="ug%d" % gi)

            def slot_dot(b, sl):
                pb = b % 2; gi = slot_gi[(b, sl)]
                ji = sl % 3
                if sl % 8 == 7:
                    stt("dve", junkb[ji][:], ug[gi][:, 0:D], 1.0, xn2b[pb][:], ALU.mult, ALU.mult,
                        r=["ug%d" % gi, "xn2b%d" % pb, "a%d_%d" % (pb, sl)], w=["a%d_%d" % (pb, sl), "junkb%d" % ji], accum=a_sb[pb][:, sl:sl + 1])
                else:
                    tt("dve", junkb[ji][:], ug[gi][:, 0:D], xn2b[pb][:], ALU.mult, r=["ug%d" % gi, "xn2b%d" % pb], w=["junkb%d" % ji])
                    act(junkb[ji][:], junkb[ji][:], AF.Copy, r=["junkb%d" % ji, "a%d_%d" % (pb, sl)], w=["junkb%d" % ji, "a%d_%d" % (pb, sl)], accum=a_sb[pb][:, sl:sl + 1])

            def slot_gelu(b, sl):
                pb = b % 2
                act(ga[:, sl:sl + 1], a_sb[pb][:, sl:sl + 1], AF.Gelu, r=["a%d_%d" % (pb, sl)], w=["ga_%d" % sl])

            def slot_acc(b, sl):
                pb = b % 2; gi = slot_gi[(b, sl)]
                di = dcount[0] % 4; dcount[0] += 1
                ts("dve", dg[di][:], id_bf[:], ga[:, sl:sl + 1], gate[pb][:, sl:sl + 1], ALU.mult, ALU.mult,
                   r=["ga_%d" % sl, "gate%d" % pb], w=["dg%d" % di])
                mm(P[5][:], dg[di][:], ug[gi][:, D:D + 512], sl == 0, sl == 127, r=["dg%d" % di, "ug%d" % gi], w=[PN[5]])
                mm(P[6][:], dg[di][:], ug[gi][:, D + 512:2 * D], sl == 0, sl == 127, r=["dg%d" % di, "ug%d" % gi], w=[PN[6]])

            def slot_loop(b, fe_ops):
                per = (len(fe_ops) + 111) // 112
                for sl in range(min(LA, 128)):
                    issue_gather(b, sl)
                for sl in range(128):
                    if sl + LA < 128:
                        issue_gather(b, sl + LA)
                    slot_dot(b, sl)
                    if sl >= 1:
                        slot_gelu(b, sl - 1)
                    if sl >= 3:
                        slot_acc(b, sl - 3)
                    S.flush(fe_ops, per)
                slot_gelu(b, 127)
                for s_ in (125, 126, 127):
                    slot_acc(b, s_)
                S.flush(fe_ops, len(fe_ops))
                oi = b % 2
                tt("dve", ob[oi][:, 0:512], P[5][:], hbuf[:, b, 0:512], ALU.add, r=[PN[5]], w=["ob%d" % oi])
                tt("dve", ob[oi][:, 512:1024], P[6][:], hbuf[:, b, 512:1024], ALU.add, r=[PN[6]], w=["ob%d" % oi])
                dma("sp", out[b * 128:(b + 1) * 128, :], ob[oi][:], r=["ob%d" % oi], key="obo%d" % oi)

            front_end(0)
            for b in range(NB):
                fe_ops = []
                if b + 1 < NB:
                    S.defer_begin()
                    front_end(b + 1)
                    fe_ops = S.defer_end()
                slot_loop(b, fe_ops)
            S.emit()
    return nc


def _t5_bucket(dist):
    dist = np.asarray(dist)
    d_f = np.maximum(dist, 1).astype(np.float32)
    large = 16 + (np.log(d_f / np.float32(16.0)) / np.float32(np.log(128.0 / 16.0)) * np.float32(16.0)).astype(np.int32)
    large = np.minimum(large, 31)
    return np.where(dist < 16, dist, large)


def prep_shared(inp):
    f = np.float32
    sh = {}
    sh["g1col"] = np.ascontiguousarray(inp["ln1_g"][0].reshape(8, 128).T)
    sh["w_in"] = np.ascontiguousarray(inp["w_in"][0])
    sh["bgcol"] = np.ascontiguousarray(inp["b_gate"][0].reshape(16, 128).T)
    lr = inp["s5_lambda_re"][0]; li = inp["s5_lambda_im"][0]; ls = inp["s5_log_step"][0]
    sh["lr_row"] = np.ascontiguousarray(np.broadcast_to(lr.reshape(1, 2048), (128, 2048)))
    sh["li_row"] = np.ascontiguousarray(np.broadcast_to(li.reshape(1, 2048), (128, 2048)))
    sh["ls_row"] = np.ascontiguousarray(np.broadcast_to(np.repeat(ls, 64).reshape(1, 2048), (128, 2048)))
    sh["lr_col"] = np.ascontiguousarray(lr.reshape(16, 2, 64).transpose(1, 2, 0).reshape(128, 16))
    sh["li_col"] = np.ascontiguousarray(li.reshape(16, 2, 64).transpose(1, 2, 0).reshape(128, 16))
    sh["ls_col"] = np.ascontiguousarray(np.broadcast_to(ls.reshape(16, 2, 1), (16, 2, 64)).transpose(1, 2, 0).reshape(128, 16))
    for nm, src in (("bT_re", inp["s5_b_re"][0]), ("bT_im", inp["s5_b_im"][0])):
        a = np.zeros((8, 16, 4, 8, 64), f)
        for g in range(32):
            a[g % 8, :, g // 8, g % 8, :] = src[g].T
        sh[nm] = a.reshape(128, 2048)
    for nm, src in (("cpad_re", inp["s5_c_re"][0]), ("cpad_im", inp["s5_c_im"][0])):
        a = np.zeros((2, 64, 16, 8, 16), f)
        for g in range(32):
            a[g % 2, :, g // 2, g % 8, :] = src[g].T
        sh[nm] = a.reshape(128, 2048)
    a = np.zeros((8, 16, 4, 8, 16), f)
    dsk = inp["s5_d"][0]
    for g in range(32):
        a[g % 8, np.arange(16), g // 8, g % 8, np.arange(16)] = dsk[g]
    sh["ddiag"] = a.reshape(128, 512)
    sh["w_glu"] = np.ascontiguousarray(inp["s5_w_glu"][0]); sh["w_s5b"] = np.ascontiguousarray(inp["w_s5_branch"][0])
    sh["qgcol"] = np.ascontiguousarray(np.tile(inp["q_norm_g"][0], 2).reshape(128, 1))
    sh["kgcol"] = np.ascontiguousarray(np.tile(inp["k_norm_g"][0], 2).reshape(128, 1))
    sh["sinks"] = np.ascontiguousarray(np.broadcast_to(inp["attn_sinks"][0].reshape(1, 16), (128, 16)))
    s_i = np.arange(128)[:, None]; q_i = np.arange(128)[None, :]
    dist = np.stack([128 + q_i - s_i, q_i - s_i], 0)
    tab = inp["rel_bias_table"]
    bias = tab[_t5_bucket(np.maximum(dist, 0))]
    bias = bias.reshape(2, 128, 128, 4, 4).transpose(1, 3, 0, 4, 2)
    sh["bias_t"] = np.ascontiguousarray(bias.reshape(128, 4, 1024)).astype(f)
    valid = (dist >= 0) & (dist < 128)
    m = np.where(valid, 0.0, NEG).astype(f)
    sh["mask_t"] = np.ascontiguousarray(np.broadcast_to(m.transpose(1, 0, 2)[:, :, None, :], (128, 2, 4, 128)).reshape(128, 1024))
    sh["w_attn"] = np.ascontiguousarray(inp["w_attn_branch"][0]); sh["w_out"] = np.ascontiguousarray(inp["w_out"][0])
    sh["g2row"] = np.ascontiguousarray(np.broadcast_to(inp["ln2_g"][0].reshape(1, D), (128, D)))
    sh["w_pq"] = np.ascontiguousarray(inp["peer_w_query"][0])
    sk = inp["peer_sub_keys"][0]
    a = np.zeros((2, 64, 8, 2, 128), f)
    for k in range(2):
        a[k, :, :, k, :] = sk[:, k].transpose(2, 0, 1)
    sh["skbd"] = a.reshape(128, 2048)
    sh["peer_u"] = np.ascontiguousarray(inp["peer_u"][0]); sh["peer_v"] = np.ascontiguousarray(inp["peer_v"][0])
    sh["ident"] = np.eye(128, dtype=f)
    sh["tri"] = (np.arange(128)[:, None] <= np.arange(128)[None, :]).astype(f)
    sh["blk1"] = ((np.arange(128)[:, None] // 64) == (np.arange(128)[None, :] // 64)).astype(f)
    sh["iota_col"] = np.arange(128, dtype=f).reshape(128, 1)
    sh["iota_row"] = np.ascontiguousarray(np.broadcast_to(np.arange(128, dtype=f).reshape(1, 128), (128, 128)))
    return sh


_NC_CACHE = {}


def kernel(**inputs):
    inp = {k: np.asarray(v) for k, v in inputs.items()}
    sh = prep_shared(inp)
    n = 8
    if "nc" not in _NC_CACHE:
        _NC_CACHE["nc"] = build_nc()
    nc = _NC_CACHE["nc"]
    in_maps = []
    for c in range(n):
        m = dict(sh)
        m["x"] = np.ascontiguousarray(inp["x"][c])
        in_maps.append(m)
    res = run_bass_kernel_spmd(nc, in_maps, core_ids=list(range(n)))
    return np.stack([np.asarray(r["out"]).reshape(T, D) for r in res.results], axis=0).astype(np.float32)
```
